# Optimizing a Trainium2 kernel written in Bass

```python
import jax, jax.numpy as jnp
from jax import lax
import numpy as np

D_MODEL = 1024
BATCH = 8
SEQ = 2048
DEPTH = 1
DEC_BATCH = 128
DEC_SEQ = 4
PAST_LEN = 16384
PAGE_SIZE = 128

A_WIDTH = D_MODEL // 2
A_HEAD_K = 128
A_HEADS = A_WIDTH // A_HEAD_K
A_HEAD_V = A_WIDTH // A_HEADS
A_CHUNK = 64
B_WIDTH = D_MODEL // 2
B_HEAD = 64
B_HEADS = B_WIDTH // B_HEAD
DECAY_LORA = 64
AAA_LORA = 64
GATE_LORA = 128
D_FF = 4 * D_MODEL
NORM_EPS = 1e-6
HGRN_NORM_EPS = 1e-5
GN_EPS = 64e-5
DECAY_SCALE = 0.6065306597126334
A_COLS = 4 * A_WIDTH
B_COLS = 3 * B_WIDTH + DECAY_LORA + AAA_LORA + GATE_LORA
GATE_COLS = 2 * D_MODEL
IN_COLS = A_COLS + B_COLS + GATE_COLS

kernel_name = "hgrn2_rwkv7_gated_hybrid_step"


def _rmsnorm(x, g, eps=NORM_EPS):
    xf = x.astype(jnp.float32)
    var = jnp.mean(jnp.square(xf), axis=-1, keepdims=True)
    return (xf * lax.rsqrt(var + eps) * g.astype(jnp.float32)).astype(x.dtype)


def _chunk_len(T):
    for c in (A_CHUNK, 32, 16, 8, 4, 2, 1):
        if T % c == 0:
            return c
    return 1


def _gla_chunked(q, k, v, logf, s0):
    B, T, H, dk = q.shape
    dv = v.shape[-1]
    C = _chunk_len(T)
    n = T // C

    def to_chunks(a):
        return a.reshape(B, n, C, H, a.shape[-1]).transpose(1, 0, 3, 2, 4)

    mask = jnp.tril(jnp.ones((C, C), dtype=bool))[:, :, None]

    def step(S, xs):
        qc, kc, vc, gc = xs
        b = jnp.cumsum(gc, axis=2)
        inter = jnp.einsum('bhtk,bhkv->bhtv', qc * jnp.exp(b), S)
        diff = b[:, :, :, None, :] - b[:, :, None, :, :]
        decay = jnp.exp(jnp.where(mask, diff, -jnp.inf))
        att = jnp.einsum('bhtk,bhtsk,bhsk->bhts', qc, decay, kc)
        intra = jnp.einsum('bhts,bhsv->bhtv', att, vc)
        b_last = b[:, :, -1:, :]
        S_new = S * jnp.exp(b_last[:, :, 0, :, None]) + jnp.einsum(
            'bhsk,bhsv->bhkv', kc * jnp.exp(b_last - b), vc)
        return S_new, inter + intra

    S, o = lax.scan(step, s0, (to_chunks(q), to_chunks(k), to_chunks(v), to_chunks(logf)))
    o = o.transpose(1, 0, 3, 2, 4).reshape(B, T, H, dv)
    return o, S


def _hgrn2_branch(u_a, lb, hgrn_norm_w, s0):
    B, T, _ = u_a.shape
    uf = u_a.astype(jnp.float32)
    q, fz, i, g = jnp.split(uf, 4, axis=-1)
    lbh = lb.reshape(A_HEADS, A_HEAD_K)
    fz = fz.reshape(B, T, A_HEADS, A_HEAD_K)
    logf = jnp.log(lbh + (1.0 - lbh) * jax.nn.sigmoid(fz))
    k = (1.0 - lbh) * jax.nn.sigmoid(-fz)
    q = jax.nn.silu(q).reshape(B, T, A_HEADS, A_HEAD_K)
    i = i.reshape(B, T, A_HEADS, A_HEAD_V)
    o, S = _gla_chunked(q, k, i, logf, s0.astype(jnp.float32))
    o = o * lax.rsqrt(jnp.mean(jnp.square(o), axis=-1, keepdims=True) + HGRN_NORM_EPS)
    o = o * hgrn_norm_w.astype(jnp.float32).reshape(A_HEADS, A_HEAD_V)
    o = o.reshape(B, T, A_WIDTH) * jax.nn.silu(g)
    return o, S


def _rwkv7_branch(u_b, u_b_prev, s0, mu_shift, w_decay0, w_decay_up, a0, w_aaa_up,
                  w_gate_up, k_k, k_a, r_k, ln_x_w, ln_x_b):
    B, T, _ = u_b.shape
    f32 = jnp.float32
    ub = u_b.astype(f32)
    up = jnp.concatenate([u_b_prev.astype(f32)[:, None, :], ub[:, :-1]], axis=1)
    xm = ub + (up - ub) * mu_shift.astype(f32)
    s1, s2, s3 = B_WIDTH, 2 * B_WIDTH, 3 * B_WIDTH
    s4, s5 = s3 + DECAY_LORA, s3 + DECAY_LORA + AAA_LORA
    r, k, v = xm[..., :s1], xm[..., s1:s2], xm[..., s2:s3]
    wd, ad, gd = xm[..., s3:s4], xm[..., s4:s5], xm[..., s5:]
    logw = -DECAY_SCALE * jax.nn.sigmoid(w_decay0.astype(f32) + jnp.tanh(wd) @ w_decay_up.astype(f32))
    a = jax.nn.sigmoid(a0.astype(f32) + ad @ w_aaa_up.astype(f32))
    g = jax.nn.sigmoid(gd) @ w_gate_up.astype(f32)
    hs = (B, T, B_HEADS, B_HEAD)
    kk = (k * k_k.astype(f32)).reshape(hs)
    kk = kk / jnp.maximum(jnp.sqrt(jnp.sum(jnp.square(kk), axis=-1, keepdims=True)), 1e-12)
    k = k * (1.0 + (a - 1.0) * k_a.astype(f32))
    r, k, v, w, a = (t.reshape(hs) for t in (r, k, v, jnp.exp(logw), a))

    def step(S, xs):
        rt, wt, kt, vt, kkt, at = xs
        sk = jnp.einsum('bhij,bhj->bhi', S, kkt)
        S = S * wt[:, :, None, :] - sk[..., None] * (kkt * at)[:, :, None, :] + vt[..., None] * kt[:, :, None, :]
        return S, jnp.einsum('bhij,bhj->bhi', S, rt)

    tr = lambda t: jnp.swapaxes(t, 0, 1)
    S, y = lax.scan(step, s0.astype(f32), (tr(r), tr(w), tr(k), tr(v), tr(kk), tr(a)))
    y = tr(y)
    mean = jnp.mean(y, axis=-1, keepdims=True)
    var = jnp.mean(jnp.square(y - mean), axis=-1, keepdims=True)
    yn = (y - mean) * lax.rsqrt(var + GN_EPS)
    yn = yn * ln_x_w.astype(f32).reshape(B_HEADS, B_HEAD) + ln_x_b.astype(f32).reshape(B_HEADS, B_HEAD)
    bonus = jnp.sum(r * k * r_k.astype(f32), axis=-1, keepdims=True) * v
    o = (yn + bonus).reshape(B, T, B_WIDTH) * g
    return o, S


def _layer(x, h_prev, s_a, s_b, lb, norm_mix_g, w_in, mu_shift, w_decay0, w_decay_up, a0,
           w_aaa_up, w_gate_up, k_k, k_a, r_k, ln_x_w, ln_x_b, hgrn_norm_w, w_a_out, w_b_out,
           w_out, norm_mlp_g, w_up, w_down):
    h = _rmsnorm(x, norm_mix_g)
    u = jnp.einsum('btd,dc->btc', h, w_in)
    u_a = u[..., :A_COLS]
    u_b = u[..., A_COLS:A_COLS + B_COLS]
    u_g = u[..., A_COLS + B_COLS:]
    u_b_prev = h_prev @ w_in[:, A_COLS:A_COLS + B_COLS]
    o_a, s_a_new = _hgrn2_branch(u_a, lb, hgrn_norm_w, s_a)
    o_b, s_b_new = _rwkv7_branch(u_b, u_b_prev, s_b, mu_shift, w_decay0, w_decay_up, a0,
                                 w_aaa_up, w_gate_up, k_k, k_a, r_k, ln_x_w, ln_x_b)
    y_a = o_a.astype(x.dtype) @ w_a_out
    y_b = o_b.astype(x.dtype) @ w_b_out
    gate_a = jax.nn.sigmoid(u_g[..., :D_MODEL])
    gate_b = jax.nn.sigmoid(u_g[..., D_MODEL:])
    x = x + (gate_a * y_a + gate_b * y_b) @ w_out
    h2 = _rmsnorm(x, norm_mlp_g)
    x = x + jnp.square(jax.nn.relu(h2 @ w_up)) @ w_down
    return x, h[:, -1, :], s_a_new, s_b_new


def _trunk(x, shift0, sa0, sb0, norm_mix_g, w_in, mu_shift, w_decay0, w_decay_up, a0, w_aaa_up,
           w_gate_up, k_k, k_a, r_k, ln_x_w, ln_x_b, lb_logits, hgrn_norm_w, w_a_out, w_b_out,
           w_out, norm_mlp_g, w_up, w_down, norm_final_g, state_dtype):
    lb_all = jnp.cumsum(jax.nn.softmax(lb_logits.astype(jnp.float32), axis=0), axis=0)
    shifts, sas, sbs = [], [], []
    for l in range(DEPTH):
        x, sh, sa, sb = _layer(
            x, shift0[l], sa0[l], sb0[l], lb_all[l], norm_mix_g[l], w_in[l], mu_shift[l],
            w_decay0[l], w_decay_up[l], a0[l], w_aaa_up[l], w_gate_up[l], k_k[l], k_a[l], r_k[l],
            ln_x_w[l], ln_x_b[l], hgrn_norm_w[l], w_a_out[l], w_b_out[l], w_out[l],
            norm_mlp_g[l], w_up[l], w_down[l])
        shifts.append(sh)
        sas.append(sa.astype(state_dtype))
        sbs.append(sb.astype(state_dtype))
    y = _rmsnorm(x, norm_final_g)
    return y, jnp.stack(sas, axis=0), jnp.stack(sbs, axis=0), jnp.stack(shifts, axis=0)


def setup_inputs(seed: int = 0) -> dict:
    key = jax.random.key(seed)
    ks = jax.random.split(key, 32)
    f32 = jnp.float32
    nrm = lambda k, shape, s: jax.random.normal(k, shape, f32) * s
    L = DEPTH
    return {
        "x_prompt": nrm(ks[0], (BATCH, SEQ, D_MODEL), 1.0),
        "x_sample": nrm(ks[1], (DEC_BATCH, DEC_SEQ, D_MODEL), 1.0),
        "state_hgrn": nrm(ks[2], (L, DEC_BATCH, A_HEADS, A_HEAD_K, A_HEAD_V), 0.3),
        "state_wkv": nrm(ks[3], (L, DEC_BATCH, B_HEADS, B_HEAD, B_HEAD), 0.3),
        "state_shift": nrm(ks[4], (L, DEC_BATCH, D_MODEL), 1.0),
        "norm_mix_g": 1.0 + nrm(ks[5], (L, D_MODEL), 0.02),
        "w_in": nrm(ks[6], (L, D_MODEL, IN_COLS), D_MODEL ** -0.5),
        "mu_shift": jax.random.uniform(ks[7], (L, B_COLS), f32),
        "w_decay0": nrm(ks[8], (L, B_WIDTH), 0.5),
        "w_decay_up": nrm(ks[9], (L, DECAY_LORA, B_WIDTH), DECAY_LORA ** -0.5),
        "a0": nrm(ks[10], (L, B_WIDTH), 0.3),
        "w_aaa_up": nrm(ks[11], (L, AAA_LORA, B_WIDTH), AAA_LORA ** -0.5),
        "w_gate_up": nrm(ks[12], (L, GATE_LORA, B_WIDTH), GATE_LORA ** -0.5),
        "k_k": 0.85 + nrm(ks[13], (L, B_WIDTH), 0.05),
        "k_a": 1.0 + nrm(ks[14], (L, B_WIDTH), 0.05),
        "r_k": nrm(ks[15], (L, B_HEADS, B_HEAD), 0.1),
        "ln_x_w": 1.0 + nrm(ks[16], (L, B_WIDTH), 0.02),
        "ln_x_b": nrm(ks[17], (L, B_WIDTH), 0.02),
        "lb_logits": nrm(ks[18], (L + 1, A_WIDTH), 0.5),
        "hgrn_norm_w": 1.0 + nrm(ks[19], (L, A_WIDTH), 0.02),
        "w_a_out": nrm(ks[20], (L, A_WIDTH, D_MODEL), A_WIDTH ** -0.5),
        "w_b_out": nrm(ks[21], (L, B_WIDTH, D_MODEL), B_WIDTH ** -0.5),
        "w_out": nrm(ks[22], (L, D_MODEL, D_MODEL), D_MODEL ** -0.5),
        "norm_mlp_g": 1.0 + nrm(ks[23], (L, D_MODEL), 0.02),
        "w_up": nrm(ks[24], (L, D_MODEL, D_FF), D_MODEL ** -0.5),
        "w_down": nrm(ks[25], (L, D_FF, D_MODEL), D_FF ** -0.5),
        "norm_final_g": 1.0 + nrm(ks[26], (D_MODEL,), 0.02),
    }


def reference(x_prompt, x_sample, state_hgrn, state_wkv, state_shift, norm_mix_g, w_in, mu_shift,
              w_decay0, w_decay_up, a0, w_aaa_up, w_gate_up, k_k, k_a, r_k, ln_x_w, ln_x_b,
              lb_logits, hgrn_norm_w, w_a_out, w_b_out, w_out, norm_mlp_g, w_up, w_down,
              norm_final_g):
    weights = (norm_mix_g, w_in, mu_shift, w_decay0, w_decay_up, a0, w_aaa_up, w_gate_up, k_k,
               k_a, r_k, ln_x_w, ln_x_b, lb_logits, hgrn_norm_w, w_a_out, w_b_out, w_out,
               norm_mlp_g, w_up, w_down, norm_final_g)
    sdt = state_hgrn.dtype
    bp = x_prompt.shape[0]
    shift0 = jnp.zeros((DEPTH, bp, D_MODEL), x_prompt.dtype)
    sa0 = jnp.zeros((DEPTH, bp, A_HEADS, A_HEAD_K, A_HEAD_V), jnp.float32)
    sb0 = jnp.zeros((DEPTH, bp, B_HEADS, B_HEAD, B_HEAD), jnp.float32)
    y_prompt, hgrn_p, wkv_p, shift_p = _trunk(x_prompt, shift0, sa0, sb0, *weights, sdt)
    y_sample, hgrn_s, wkv_s, shift_s = _trunk(x_sample, state_shift, state_hgrn, state_wkv,
                                              *weights, sdt)
    return (y_prompt, y_sample, hgrn_p, wkv_p, shift_p, hgrn_s, wkv_s, shift_s)
```

```python
import contextlib
import numpy as np
import concourse.bass as bass
import concourse.mybir as mybir
from concourse.bass_utils import run_bass_kernel_spmd

F32 = mybir.dt.float32
BF16 = mybir.dt.bfloat16
AF = mybir.ActivationFunctionType
ALU = mybir.AluOpType
AX = mybir.AxisListType

D = 1024
NCORE = 8
SEQ = 2048
NSB = 16
DECAY_SCALE = 0.6065306597126334
NORM_EPS = 1e-6
HGRN_EPS = 1e-5
GN_EPS = 64e-5


class _Instr:
    __slots__ = ("eng", "fn", "deps", "dma_key", "dma_val", "signal", "sem_val")

    def __init__(self, eng, fn):
        self.eng = eng
        self.fn = fn
        self.deps = []
        self.dma_key = None
        self.dma_val = 0
        self.signal = False
        self.sem_val = 0


class Prog:
    ENGS = ("pe", "act", "dve", "pool", "sp")

    def __init__(self, nc):
        self.nc = nc
        self.streams = {e: [] for e in self.ENGS}
        self.last_writer = {}
        self.readers = {}
        self.dma_count = {}
        self.dma_last = {}
        self.final_waits = []

    def _deps(self, ins, reads, writes):
        deps = ins.deps
        for s in reads:
            w = self.last_writer.get(s)
            if w is not None:
                deps.append(w)
        for s in writes:
            w = self.last_writer.get(s)
            if w is not None:
                deps.append(w)
            deps.extend(self.readers.get(s, ()))
        for s in reads:
            self.readers.setdefault(s, []).append(ins)
        for s in writes:
            self.last_writer[s] = ins
            self.readers[s] = []

    def op(self, eng, fn, reads=(), writes=()):
        ins = _Instr(eng, fn)
        self._deps(ins, reads, writes)
        self.streams[eng].append(ins)
        return ins

    def dma(self, eng, out, in_, reads=(), writes=(), key=None, final=False, **kw):
        ins = _Instr(eng, lambda e: e.dma_start(out=out, in_=in_, **kw))
        ins.dma_key = key
        self.dma_count[key] = self.dma_count.get(key, 0) + 1
        ins.dma_val = 16 * self.dma_count[key]
        self.dma_last[key] = ins
        self._deps(ins, reads, writes)
        self.streams[eng].append(ins)
        if final:
            self.final_waits.append(key)
        return ins

    def barrier(self):
        lasts = []
        for e in ("pe", "act", "dve", "pool"):
            for i in reversed(self.streams[e]):
                if i.dma_key is None and i.fn is not None:
                    lasts.append(i)
                    break
        dm = list(self.dma_last.values())
        for e in ("pe", "act", "dve", "pool", "sp"):
            ins = _Instr(e, None)
            ins.deps = list(lasts) + dm
            self.streams[e].append(ins)
        self.last_writer = {}
        self.readers = {}

    def emit(self):
        nc = self.nc
        for e in self.ENGS:
            for ins in self.streams[e]:
                for d in ins.deps:
                    if d.dma_key is None and d.fn is not None:
                        d.signal = True
        for e in self.ENGS:
            c = 0
            for ins in self.streams[e]:
                if ins.dma_key is None and ins.signal:
                    c += 1
                    ins.sem_val = c
        with contextlib.ExitStack() as st:
            esem = {e: st.enter_context(nc.semaphore(f"s_{e}")) for e in ("pe", "act", "dve", "pool")}
            dsem = {k: st.enter_context(nc.semaphore(f"d_{i}")) for i, k in enumerate(self.dma_count)}
            block = st.enter_context(nc.Block())

            def run(ename):
                def body(eng):
                    waited = {}
                    for ins in self.streams[ename]:
                        need = {}
                        for d in ins.deps:
                            if d.dma_key is not None:
                                sem, val, k = dsem[d.dma_key], d.dma_val, ("d", d.dma_key)
                            else:
                                if d.eng == ename and ename == "pe":
                                    continue
                                sem, val, k = esem[d.eng], d.sem_val, ("e", d.eng)
                            if waited.get(k, 0) >= val:
                                continue
                            if k not in need or need[k][1] < val:
                                need[k] = (sem, val)
                        for k, (sem, val) in need.items():
                            eng.wait_ge(sem, val)
                            waited[k] = val
                        if ins.fn is None:
                            continue
                        bi = ins.fn(eng)
                        if ins.dma_key is not None:
                            bi.then_inc(dsem[ins.dma_key], 16)
                        elif ins.signal:
                            bi.then_inc(esem[ename], 1)
                    if ename == "sp":
                        for key in self.final_waits:
                            eng.wait_ge(dsem[key], 16 * self.dma_count[key])
                return body

            block.sync(run("sp"))
            block.tensor(run("pe"))
            block.scalar(run("act"))
            block.vector(run("dve"))
            block.gpsimd(run("pool"))


C_ID = 0
C_MIT = 128
C_MST = 256
C_ML = 384
C_SIT = 512
C_SST = 640
C_SL = 768
C_ONESBLK = 896
C_HEADSEL = 1024
C_SELQ = 1026
C_SELV = 2050
C_ESEL = 2066
C_HSEL4 = 2066 + 1024
NCST = 2066 + 1024 + 256
PV_L0, PV_L1, PV_MU, PV_W0, PV_A0, PV_KK, PV_KA, PV_RK = 0, 4, 8, 22, 26, 30, 34, 38
NPV = 42
PB_GMIX, PB_GMLP, PB_GFIN, PB_HNW, PB_LXW, PB_LXB = 0, 1024, 2048, 3072, 3584, 4096
NPB = 4608


def _build_consts():
    c = np.zeros((128, NCST), np.float32)
    i = np.arange(128)
    c[:, C_ID:C_ID + 128] = np.eye(128)
    same = (i[:, None] // 64) == (i[None, :] // 64)
    c[:, C_MIT:C_MIT + 128] = same & (i[:, None] <= i[None, :])
    c[:, C_MST:C_MST + 128] = same & (i[:, None] < i[None, :])
    c[:, C_ML:C_ML + 128] = same & (i[None, :] < i[:, None])
    j = np.arange(64)
    s4 = (j[:, None] // 4) == (j[None, :] // 4)
    c[:64, C_SIT:C_SIT + 64] = s4 & (j[:, None] <= j[None, :])
    c[:64, C_SST:C_SST + 64] = s4 & (j[:, None] < j[None, :])
    c[:64, C_SL:C_SL + 64] = s4 & (j[None, :] < j[:, None])
    c[:, C_ONESBLK:C_ONESBLK + 128] = same
    c[:64, C_HEADSEL] = 1.0
    c[64:, C_HEADSEL + 1] = 1.0
    selq = (np.arange(16)[:, None] == (j[None, :] // 4)).astype(np.float32)
    c[:, C_SELQ:C_SELQ + 1024] = selq.reshape(1, 1024)
    c[:64, C_SELV:C_SELV + 16] = ((j[:, None] // 4) == np.arange(16)[None, :])
    esel = np.zeros((64, 16, 64), np.float32)
    for b in range(16):
        esel[4 * b + 3, b, :] = 1.0
    c[:64, C_ESEL:C_ESEL + 1024] = esel.reshape(64, 1024)
    for jj in range(4):
        c[:64, C_HSEL4 + jj * 64 + 2 * jj] = 1.0
        c[64:, C_HSEL4 + jj * 64 + 2 * jj + 1] = 1.0
    cf = np.zeros((128, 1792), np.float32)
    cm = np.ones(512, np.float32); cm[::64] = 0.0
    cf[:, 0:512] = cm
    cs = np.ones(64, np.float32); cs[::4] = 0.0
    cf[:, 512:576] = cs
    cf[:, 640:768] = np.eye(128)
    cf[:64, 768:1792] = esel.reshape(64, 1024)
    return c, cf


DBG = {}


def build_nc(blocks, last_pt=15, stop_after=None):
    nc = bass.Bass("TRN2", target_bir_lowering=False)
    din = lambda n, s: nc.dram_tensor(n, s, F32, kind="ExternalInput").ap()
    dout = lambda n, s: nc.dram_tensor(n, s, F32, kind="ExternalOutput").ap()
    xp = din("xp", [SEQ, D]); xs = din("xs", [64, D]); shs = din("shs", [NSB, D])
    shg = din("shg", [NSB, 4, 128, 128]); swk = din("swk", [NSB, 8, 64, 64])
    w_in = din("w_in", [D, 5888]); w_a_out = din("w_a_out", [512, D]); w_b_out = din("w_b_out", [512, D])
    w_out = din("w_out", [D, D]); w_up = din("w_up", [D, 4096]); w_down = din("w_down", [4096, D])
    wlora_d = din("wlora", [128, 512]); wgu_d = din("wgu", [128, 512])
    pv_d = din("pv", [128, NPV]); pb_d = din("pb", [128, NPB])
    cst_d = din("cst", [128, NCST]); cstf_d = din("cstf", [128, 1792])
    y_p = dout("y_p", [SEQ, D]); y_s = dout("y_s", [64, D])
    hg_p = dout("hg_p", [4, 128, 128]); wk_p = dout("wk_p", [8, 64, 64]); sh_p = dout("sh_p", [1, D])
    hg_s = dout("hg_s", [NSB, 4, 128, 128]); wk_s = dout("wk_s", [NSB, 8, 64, 64]); sh_s = dout("sh_s", [NSB, D])

    w_in_v = w_in.rearrange("(kt p) c -> p kt c", p=128)
    w_up_v = w_up.rearrange("(kt p) c -> p kt c", p=128)
    w_dn_v = w_down.rearrange("(kt p) c -> p kt c", p=128)
    w_ao_v = w_a_out.rearrange("(kt p) c -> p kt c", p=128)
    w_bo_v = w_b_out.rearrange("(kt p) c -> p kt c", p=128)
    w_o_v = w_out.rearrange("(kt p) c -> p kt c", p=128)

    MAXC = 592
    with contextlib.ExitStack() as st:
        sb = lambda n, s, d: st.enter_context(nc.sbuf_tensor("sb_" + n, s, d))
        P = Prog(nc)
        cst = sb("cst", [128, NCST], BF16)
        cstf = sb("cstf", [128, 1792], F32)
        pv = sb("pv", [128, NPV + 16], F32)
        pbc = sb("pbc", [128, NPB], F32)
        wlora = sb("wlora", [128, 512], BF16)
        wgu = sb("wgu", [128, 512], BF16)
        ring = [sb(f"ring{i}", [128, 8192], BF16) for i in range(4)]
        hT = sb("hT", [128, 8, MAXC], BF16)
        oT = sb("oT", [128, 8, MAXC], BF16)
        Sh = sb("Sh", [128, 4, 128], F32)
        Shb = [sb(f"Shb{i}", [128, 4, 128], BF16) for i in range(2)]
        Hr = sb("Hr", [128, 4, 64], F32)
        Hrb = [sb(f"Hrb{i}", [128, 4, 64], BF16) for i in range(2)]
        carry = sb("carry", [128, 16], F32)
        small = sb("small", [128, 64], F32)
        scr = sb("scr", [128, 21504], F32)
        pst = [st.enter_context(nc.psum_tensor(f"ps{i}", [128, 512], F32)) for i in range(8)]
        psb = [p[:].bitcast(BF16) for p in pst]

        ident = cst[:, C_ID:C_ID + 128]
        identf = cstf[:, 640:768]

        class Arena:
            def __init__(self):
                self.off = 0

            def f32(self, n):
                a = scr[:, self.off:self.off + n]
                self.off += n
                assert self.off <= 21504, self.off
                return a

            def bf(self, n):
                m = (n + 1) // 2
                a = scr[:, self.off:self.off + m].bitcast(BF16)
                self.off += m
                assert self.off <= 21504, self.off
                return a

        def mm(out, lhsT, rhs, start, stop, r, w):
            P.op("pe", lambda e: e.matmul(out, lhsT, rhs, start=start, stop=stop), r, w)

        def tr(out, in_, idn, r, w):
            P.op("pe", lambda e: e.transpose(out, in_, idn), r, w)

        def act(out, in_, func, r, w, bias=0.0, scale=1.0, accum_out=None):
            if accum_out is None:
                P.op("act", lambda e: e.activation(out, in_, func, bias=bias, scale=scale), r, w)
            else:
                P.op("act", lambda e: e.activation(out, in_, func, bias=bias, scale=scale, accum_out=accum_out), r, w)

        def tt(out, a, b, op, r, w, eng="dve"):
            P.op(eng, lambda e: e.tensor_tensor(out, a, b, op), r, w)

        def ts(out, a, s1, s2, op0, op1, r, w, eng="dve"):
            if s2 is None:
                P.op(eng, lambda e: e.tensor_scalar(out, a, s1, None, op0), r, w)
            else:
                P.op(eng, lambda e: e.tensor_scalar(out, a, s1, s2, op0, op1), r, w)

        def stt(out, a, s, b, op0, op1, r, w, eng="dve"):
            P.op(eng, lambda e: e.scalar_tensor_tensor(out, a, s, b, op0, op1), r, w)

        def cp(out, in_, r, w, eng="dve"):
            if eng == "act":
                P.op("act", lambda e: e.copy(out, in_), r, w)
            else:
                P.op(eng, lambda e: e.tensor_copy(out, in_), r, w)

        def rsqrt_small(out, in_, scale, eps, r, w):
            act(out, in_, AF.Ln, r, w, bias=eps, scale=scale)
            act(out, out, AF.Exp, w, w, scale=-0.5)

        P.dma("pool", cst[:], cst_d, writes=["cst"], key="cst")
        P.dma("sp", cstf[:], cstf_d, writes=["cstf"], key="cstf")
        P.dma("sp", pv[:, 0:NPV], pv_d, writes=["pv"], key="pv")
        P.dma("sp", pbc[:], pb_d, writes=["pbc"], key="pbc")
        P.dma("pool", wlora[:], wlora_d, writes=["wlora"], key="wlora")
        P.dma("pool", wgu[:], wgu_d, writes=["wgu"], key="wgu")
        LB, OML, NOML = NPV, NPV + 4, NPV + 8
        tt(pv[:, LB:LB + 4], pv[:, PV_L1:PV_L1 + 4], pv[:, PV_L0:PV_L0 + 4], ALU.subtract, ["pv"], ["pv"])
        act(pv[:, LB:LB + 4], pv[:, LB:LB + 4], AF.Exp, ["pv"], ["pv"])
        ts(pv[:, LB:LB + 4], pv[:, LB:LB + 4], 1.0, None, ALU.add, None, ["pv"], ["pv"])
        P.op("dve", lambda e: e.reciprocal(pv[:, LB:LB + 4], pv[:, LB:LB + 4]), ["pv"], ["pv"])
        ts(pv[:, OML:OML + 4], pv[:, LB:LB + 4], -1.0, 1.0, ALU.mult, ALU.add, ["pv"], ["pv"])
        ts(pv[:, NOML:NOML + 4], pv[:, OML:OML + 4], -1.0, None, ALU.mult, None, ["pv"], ["pv"])
        P.op("dve", lambda e: e.memset(Sh[:], 0.0), [], ["Sh"])
        P.op("dve", lambda e: e.memset(Shb[0][:], 0.0), [], ["Shb0"])
        P.op("dve", lambda e: e.memset(Hr[:], 0.0), [], ["Hr"])
        P.op("dve", lambda e: e.memset(Hrb[0][:], 0.0), [], ["Hrb0"])
        P.op("dve", lambda e: e.memset(carry[:], 0.0), [], ["carry"])
        state = {"shp": 0, "hrp": 0, "piece": 0}

        def piece_dmas(pi):
            if pi < 6:
                lo = [0, 1024, 2048, 3072, 3840, 4864][pi]
                n = [1024, 1024, 1024, 768, 1024, 1024][pi]
                return [(lambda R, n=n: R[:, 0:8 * n].rearrange("p (k c) -> p k c", k=8), w_in_v[:, :, lo:lo + n])]
            if pi == 6:
                return [(lambda R: R[:, 0:4096].rearrange("p (k c) -> p k c", k=4), w_ao_v),
                        (lambda R: R[:, 4096:8192].rearrange("p (k c) -> p k c", k=4), w_bo_v)]
            if pi == 7:
                return [(lambda R: R[:, 0:8192].rearrange("p (k c) -> p k c", k=8), w_o_v)]
            e8 = pi - 8
            return [(lambda R: R[:, 0:4096].rearrange("p (k c) -> p k c", k=8), w_up_v[:, :, e8 * 512:(e8 + 1) * 512]),
                    (lambda R: R[:, 4096:8192].rearrange("p (k c) -> p k c", k=4), w_dn_v[:, e8 * 4:(e8 + 1) * 4, :])]

        issued = {"n": 0}
        NPIECE = 16
        total_pieces = NPIECE * len(blocks)

        def ensure_issued(upto):
            while issued["n"] <= min(upto, total_pieces - 1):
                g = issued["n"]
                slot = g % 4
                for vf, src in piece_dmas(g % NPIECE):
                    P.dma("pool", vf(ring[slot]), src, writes=[f"ring{slot}"], key=f"ring{slot}")
                issued["n"] += 1

        def use_piece(bi, pi):
            g = bi * NPIECE + pi
            ensure_issued(g)
            return ring[g % 4], f"ring{g % 4}"

        def done_piece(bi, pi):
            ensure_issued(bi * NPIECE + pi + 4)

        ensure_issued(3)

        for bi, tiles in enumerate(blocks):
            ptiles = [t for t in tiles if t < 16]
            has_s = 16 in tiles
            npc = 128 * len(ptiles)
            SC0 = npc
            HC0 = npc + 64
            ncol = npc + (80 if has_s else 0)
            ttiles = [(t, i * 128, 128) for i, t in enumerate(ptiles)] + ([(16, SC0, 64)] if has_s else [])

            P.barrier()
            A = Arena()
            xin = [A.f32(1024) for _ in range(2)]
            hb = [A.bf(1024) for _ in range(2)]
            junk = A.bf(1024)
            hf = A.f32(1024)
            for i, (t, c0, rows) in enumerate(ttiles):
                xi, hbi = xin[i % 2], hb[i % 2]
                src = xp[t * 128:(t + 1) * 128, :] if t < 16 else xs
                P.dma("sp", xi[0:rows, :], src, writes=[f"xin{i%2}"], key=f"xin{i%2}")
                act(junk[0:rows, :], xi[0:rows, :], AF.Square, [f"xin{i%2}"], ["junk", "ssq"], accum_out=small[0:rows, 0:1])
                rsqrt_small(small[0:rows, 1:2], small[0:rows, 0:1], 1.0 / D, NORM_EPS, ["ssq"], ["rstd"])
                stt(hbi[0:rows, :], xi[0:rows, :], small[0:rows, 1:2], pbc[0:rows, PB_GMIX:PB_GMIX + D], ALU.mult, ALU.mult,
                    [f"xin{i%2}", "rstd", "pbc"], [f"hb{i%2}"])
                if t >= last_pt:
                    stt(hf[0:rows, :], xi[0:rows, :], small[0:rows, 1:2], pbc[0:rows, PB_GMIX:PB_GMIX + D], ALU.mult, ALU.mult,
                        [f"xin{i%2}", "rstd", "pbc"], ["hf"])
                    if t == last_pt:
                        P.dma("sp", sh_p, hf[127:128, :], reads=["hf"], key="o_shp", final=True)
                    else:
                        P.dma("sp", sh_s, hf[3:64:4, :], reads=["hf"], key="o_shs", final=True)
                bank = i % 2
                for k in range(8):
                    tr(psb[bank][:, k * 128:k * 128 + rows], hbi[0:rows, k * 128:(k + 1) * 128], ident[0:rows, 0:rows],
                       [f"hb{i%2}", "cst"], [f"ps{bank}"])
                cp(hT[:, :, c0:c0 + rows], psb[bank].rearrange("p (k c) -> p k c", k=8)[:, :, 0:rows], [f"ps{bank}"], ["hT"], eng="act")
            if has_s:
                P.dma("sp", xin[0][0:16, :], shs, writes=["xin0"], key="xin0")
                cp(hb[0][0:16, :], xin[0][0:16, :], ["xin0"], ["hb0"])
                for k in range(8):
                    tr(psb[0][:, k * 128:k * 128 + 16], hb[0][0:16, k * 128:(k + 1) * 128], ident[0:16, 0:16], ["hb0", "cst"], ["ps0"])
                cp(hT[:, :, HC0:HC0 + 16], psb[0].rearrange("p (k c) -> p k c", k=8)[:, :, 0:16], ["ps0"], ["hT"], eng="act")

            if stop_after == 'P0':
                break
            P.barrier()
            A = Arena()
            qtT = A.bf(4 * MAXC).rearrange("p (j c) -> p j c", j=4)
            ktT = A.bf(4 * MAXC).rearrange("p (j c) -> p j c", j=4)
            khT = A.bf(4 * MAXC).rearrange("p (j c) -> p j c", j=4)
            gam = A.f32(4 * 32).rearrange("p (j c) -> p j c", j=4)
            sg = A.f32(512); lg = A.f32(512); kf = A.f32(512); bb = A.f32(512); Ee = A.f32(512); Ei = A.f32(512); qs = A.f32(512)
            v_tm = A.bf(512); kh_tm = A.bf(512); gs = A.f32(512); att_bf = A.bf(512).rearrange("p (j c) -> p j c", j=4)
            otmp = A.f32(512); oa_tm = A.bf(512)
            s0f = A.f32(2048).rearrange("p (b v) -> p b v", b=16); s0b = A.bf(2048).rearrange("p (b v) -> p b v", b=16)
            qmk = A.bf(1024).rearrange("p (b t) -> p b t", b=16); vbd = A.bf(2048).rearrange("p (b v) -> p b v", b=16)
            W0, W0s = use_piece(bi, 0)
            wA0 = W0[:, 0:8192].rearrange("p (k c) -> p k c", k=8)
            groups = [(c, min(c + 512, npc), False) for c in range(0, npc, 512)] + ([(SC0, SC0 + 64, True)] if has_s else [])
            chunk_base = {}
            cb = 0
            for (c0, c1, is_s) in groups:
                n = c1 - c0
                csz = 4 if is_s else 64
                nch = n // csz
                cmask = cstf[:, 512:576] if is_s else cstf[:, 0:n]
                chunk_base[c0] = cb
                for j in range(4):
                    pa = pst[j % 2]
                    for k in range(8):
                        mm(pa[:, 0:n], wA0[:, k, 512 + j * 128:512 + (j + 1) * 128], hT[:, k, c0:c1], k == 0, k == 7, ["hT", W0s], [f"ps{j%2}"])
                    act(sg[:, 0:n], pa[:, 0:n], AF.Sigmoid, [f"ps{j%2}"], ["sg"])
                    act(lg[:, 0:n], sg[:, 0:n], AF.Ln, ["sg", "pv"], ["lg"], bias=pv[:, LB + j:LB + j + 1], scale=pv[:, OML + j:OML + j + 1])
                    ts(kf[:, 0:n], sg[:, 0:n], pv[:, NOML + j:NOML + j + 1], pv[:, OML + j:OML + j + 1], ALU.mult, ALU.add, ["sg", "pv"], ["kf"])
                    P.op("dve", lambda e, n=n, cmask=cmask: e.tensor_tensor_scan(bb[:, 0:n], cmask, lg[:, 0:n], 0.0, ALU.mult, ALU.add),
                         ["lg", "cstf"], ["bb"])
                    act(Ee[:, 0:n], bb[:, 0:n], AF.Exp, ["bb"], ["Ee"])
                    act(Ei[:, 0:n], bb[:, 0:n], AF.Exp, ["bb"], ["Ei"], scale=-1.0)
                    tt(ktT[:, j, c0:c1], kf[:, 0:n], Ei[:, 0:n], ALU.mult, ["kf", "Ei"], ["ktT"])
                    Ev = Ee[:, 0:n].rearrange("p (c t) -> p c t", t=csz)
                    tt(khT[:, j, c0:c1].rearrange("p (c t) -> p c t", t=csz), ktT[:, j, c0:c1].rearrange("p (c t) -> p c t", t=csz),
                       Ev[:, :, csz - 1:csz].to_broadcast([128, nch, csz]), ALU.mult, ["ktT", "Ee"], ["khT"])
                    cp(gam[:, j, cb:cb + nch], Ev[:, :, csz - 1], ["Ee"], ["gam"])
                    pq = pst[2 + j % 2]
                    for k in range(8):
                        mm(pq[:, 0:n], wA0[:, k, j * 128:(j + 1) * 128], hT[:, k, c0:c1], k == 0, k == 7, ["hT", W0s], [f"ps{2+j%2}"])
                    act(qs[:, 0:n], pq[:, 0:n], AF.Silu, [f"ps{2+j%2}"], ["qs"])
                    tt(qtT[:, j, c0:c1], qs[:, 0:n], Ee[:, 0:n], ALU.mult, ["qs", "Ee"], ["qtT"])
                cb += nch
            done_piece(bi, 0)
            W1, W1s = use_piece(bi, 1)
            wA1 = W1[:, 0:8192].rearrange("p (k c) -> p k c", k=8)
            mIT_p = cst[:, C_MIT:C_MIT + 128]
            for (t, c0, rows) in ttiles:
                is_s = t == 16
                R0 = slice(0, rows)
                for k in range(8):
                    mm(pst[0][R0, :], hT[:, k, c0:c0 + rows], wA1[:, k, 0:512], k == 0, k == 7, ["hT", W1s], ["ps0"])
                cp(v_tm[R0, :], pst[0][R0, :], ["ps0"], ["v_tm"], eng="act")
                for k in range(8):
                    mm(pst[1][R0, :], hT[:, k, c0:c0 + rows], wA1[:, k, 512:1024], k == 0, k == 7, ["hT", W1s], ["ps1"])
                act(gs[R0, :], pst[1][R0, :], AF.Silu, ["ps1"], ["gs"])
                tt(gs[R0, :], gs[R0, :], pbc[R0, PB_HNW:PB_HNW + 512], ALU.mult, ["gs", "pbc"], ["gs"])
                for j in range(4):
                    tr(psb[2][R0, j * 128:(j + 1) * 128], khT[:, j, c0:c0 + rows], ident, ["khT", "cst"], ["ps2"])
                cp(kh_tm[R0, :], psb[2][R0, 0:512], ["ps2"], ["kh_tm"], eng="act")
                p3 = pst[3][:].rearrange("p (j c) -> p j c", j=4)
                for j in range(4):
                    mm(p3[R0, j, 0:rows], ktT[:, j, c0:c0 + rows], qtT[:, j, c0:c0 + rows], True, True, ["ktT", "qtT"], ["ps3"])
                msk = cst[R0, C_SIT:C_SIT + 64] if is_s else mIT_p
                tt(att_bf[R0, :, 0:rows], p3[R0, :, 0:rows], msk.rearrange("p (o c) -> p o c", o=1).to_broadcast([rows, 4, rows]), ALU.mult,
                   ["ps3", "cst"], ["att_bf"])
                p4 = pst[4][:].rearrange("p (j c) -> p j c", j=4)
                if not is_s:
                    cbase = chunk_base[(c0 // 512) * 512] + (c0 % 512) // 64
                    p5 = pst[5][:].rearrange("p (j c) -> p j c", j=4)
                    p6 = pst[6][:].rearrange("p (j c) -> p j c", j=4)
                    for ci, pp, pn in ((0, p5, "ps5"), (1, p6, "ps6")):
                        rs = slice(ci * 64, ci * 64 + 64)
                        for j in range(4):
                            mm(pp[:, j, :], kh_tm[rs, j * 128:(j + 1) * 128], v_tm[rs, j * 128:(j + 1) * 128], True, True, ["kh_tm", "v_tm"], [pn])
                    for ci, pp, pn in ((0, p5, "ps5"), (1, p6, "ps6")):
                        rs = slice(ci * 64, ci * 64 + 64)
                        cur = state["shp"]
                        for j in range(4):
                            mm(p4[rs, j, :], qtT[:, j, c0 + ci * 64:c0 + ci * 64 + 64], Shb[cur][:, j, :], True, False, ["qtT", f"Shb{cur}"], ["ps4"])
                            mm(p4[rs, j, :], att_bf[:, j, ci * 64:ci * 64 + 64], v_tm[:, j * 128:(j + 1) * 128], False, True, ["att_bf", "v_tm"], ["ps4"])
                        for j in range(4):
                            stt(Sh[:, j, :], Sh[:, j, :], gam[:, j, cbase + ci:cbase + ci + 1], pp[:, j, :], ALU.mult, ALU.add, ["Sh", "gam", pn], ["Sh"])
                        nxt = 1 - cur
                        cp(Shb[nxt][:], Sh[:], ["Sh"], [f"Shb{nxt}"], eng="act")
                        state["shp"] = nxt
                else:
                    sb0 = chunk_base[SC0]
                    for j in range(4):
                        P.dma("sp", s0f[:], shg[:, j, :, :].rearrange("b k v -> k b v"), writes=["s0f"], key="s0f")
                        cp(s0b[:], s0f[:], ["s0f"], ["s0b"], eng="act")
                        tt(qmk[:], qtT[:, j, c0:c0 + 64].rearrange("p (o t) -> p o t", o=1).to_broadcast([128, 16, 64]),
                           cst[:, C_SELQ:C_SELQ + 1024].rearrange("p (b t) -> p b t", b=16), ALU.mult, ["qtT", "cst"], ["qmk"])
                        for b in range(16):
                            mm(p4[0:64, j, :], qmk[:, b, :], s0b[:, b, :], b == 0, False, ["qmk", "s0b"], ["ps4"])
                        mm(p4[0:64, j, :], att_bf[0:64, j, 0:64], v_tm[0:64, j * 128:(j + 1) * 128], False, True, ["att_bf", "v_tm"], ["ps4"])
                        tt(vbd[0:64], v_tm[0:64, j * 128:(j + 1) * 128].rearrange("p (o v) -> p o v", o=1).to_broadcast([64, 16, 128]),
                           cst[0:64, C_SELV:C_SELV + 16].rearrange("p (b o) -> p b o", o=1).to_broadcast([64, 16, 128]), ALU.mult, ["v_tm", "cst"], ["vbd"])
                        tt(s0f[:], s0f[:], gam[:, j, sb0:sb0 + 16].rearrange("p (b o) -> p b o", o=1).to_broadcast([128, 16, 128]), ALU.mult,
                           ["s0f", "gam", "s0b"], ["s0f"])
                        for q in range(4):
                            pq_ = pst[5 + q % 2]
                            mm(pq_[:, :], kh_tm[0:64, j * 128:(j + 1) * 128], vbd[0:64, q * 4:(q + 1) * 4, :], True, True, ["kh_tm", "vbd"], [f"ps{5+q%2}"])
                            tt(s0f[:, q * 4:(q + 1) * 4, :], s0f[:, q * 4:(q + 1) * 4, :], pq_[:].rearrange("p (b v) -> p b v", b=4), ALU.add,
                               ["s0f", f"ps{5+q%2}"], ["s0f"])
                        P.dma("sp", hg_s[:, j, :, :].rearrange("b k v -> k b v"), s0f[:], reads=["s0f"], key="o_hgs", final=True)
                for j in range(4):
                    act(otmp[R0, j * 128:(j + 1) * 128], p4[R0, j, :], AF.Square, ["ps4"], ["otmp", "ss4"], accum_out=small[R0, 8 + j:9 + j])
                rsqrt_small(small[R0, 12:16], small[R0, 8:12], 1.0 / 128, HGRN_EPS, ["ss4"], ["rs4"])
                tt(otmp[R0, :].rearrange("p (j c) -> p j c", j=4), p4[R0], small[R0, 12:16].rearrange("p (j o) -> p j o", o=1).to_broadcast([rows, 4, 128]),
                   ALU.mult, ["ps4", "rs4", "otmp"], ["otmp"])
                tt(oa_tm[R0, :], otmp[R0, :], gs[R0, :], ALU.mult, ["otmp", "gs"], ["oa_tm"])
                for j in range(4):
                    tr(psb[7][:, j * 128:j * 128 + rows], oa_tm[R0, j * 128:(j + 1) * 128], ident[R0, R0], ["oa_tm", "cst"], ["ps7"])
                cp(oT[:, 0:4, c0:c0 + rows], psb[7][:, 0:512].rearrange("p (j c) -> p j c", j=4)[:, :, 0:rows], ["ps7"], ["oT"], eng="act")
            done_piece(bi, 1)
            if bi == len(blocks) - 1:
                P.dma("sp", hg_p.rearrange("h k v -> k h v"), Sh[:], reads=["Sh"], key="o_hgp", final=True)
            if stop_after == 'P1':
                break
            import os as _os
            P.barrier()
            A = Arena()
            GN = 128
            ubj = [A.f32(GN + 1) for _ in range(2)]
            dtmp = A.f32(GN); upj = A.f32(64)
            ush = A.f32(14 * 16).rearrange("p (j c) -> p j c", j=14)
            xr = A.f32(4 * GN).rearrange("p (j c) -> p j c", j=4)
            xk = A.f32(4 * GN).rearrange("p (j c) -> p j c", j=4)
            xv = A.f32(4 * GN).rearrange("p (j c) -> p j c", j=4)
            xwa = A.f32(GN); xg = A.f32(GN)
            twa = A.bf(GN); sgd = A.bf(GN)
            sgw = A.f32(GN); cs_ = A.f32(GN); Gt = A.f32(GN); Gi = A.f32(GN); Gex = A.f32(GN); aa = A.f32(GN)
            kkf = A.f32(GN); kk2 = A.bf(GN); rn = A.f32(GN); kkn = A.f32(GN); t1 = A.f32(GN); km = A.f32(GN); ka_ = A.f32(GN)
            mk4 = lambda: A.bf(4 * GN).rearrange("p (j c) -> p j c", j=4)
            pT = mk4(); btT = mk4(); kt2 = mk4(); bhT = mk4(); khT2 = mk4(); vT = mk4()
            atT_m = [mk4(), mk4()]; rtT_m = [mk4(), mk4()]
            GtT = A.f32(4 * 64).rearrange("p (j c) -> p j c", j=4)
            gC = A.f32(4 * 16).rearrange("p (j c) -> p j c", j=4)
            v2 = A.bf(512); g_tm = A.f32(512); bsum = A.f32(8)
            bh_c = [A.bf(512), A.bf(512)]; kh_c = [A.bf(512), A.bf(512)]
            mk44 = lambda: A.bf(512).rearrange("p (h c) -> p h c", h=4)
            Lb = [mk44() for _ in range(2)]; Mb = [mk44() for _ in range(2)]; Rb = [mk44() for _ in range(2)]
            TTb = [mk44() for _ in range(2)]; AakT = [mk44() for _ in range(2)]; ArbT = [mk44() for _ in range(2)]; ArkT = [mk44() for _ in range(2)]
            Xb = A.bf(512).rearrange("p (h c) -> p h c", h=8)
            Ub = A.bf(512).rearrange("p (h c) -> p h c", h=8)
            y_tm = A.f32(512); ysq = A.f32(512); ob_tm = A.bf(512)
            st8 = A.f32(40)
            if has_s:
                s0n = A.f32(1024).rearrange("p (b k) -> p b k", b=16)
                s0c = A.bf(1024).rearrange("p (b k) -> p b k", b=16)
                hsb = A.bf(4096).rearrange("p (j b k) -> p j b k", j=4, b=16)
                atm = A.bf(1024).rearrange("p (b k) -> p b k", b=16)
                rtm = A.bf(1024).rearrange("p (b k) -> p b k", b=16)
                bdb = A.bf(1024).rearrange("p (b k) -> p b k", b=16)
                bdk = A.bf(1024).rearrange("p (b k) -> p b k", b=16)
                G_tm = A.f32(512)
            for _t, _n in ((atT_m[0], 'atT'), (atT_m[1], 'atT'), (rtT_m[0], 'rtT'), (rtT_m[1], 'rtT')):
                P.op('dve', lambda e, _t=_t: e.memset(_t, 0.0), [], [_n])
            for _t, _n in ((bh_c[0], 'bh_tm'), (bh_c[1], 'bh_tm'), (kh_c[0], 'kh2_tm'), (kh_c[1], 'kh2_tm'), (Xb, 'Xb'), (Ub, 'Ub')):
                P.op('dve', lambda e, _t=_t: e.memset(_t, 0.0), [], [_n])
            if has_s:
                P.op('dve', lambda e: e.memset(hsb, 0.0), [], ['hsb'])
            W2, W2s = use_piece(bi, 2)
            W3, W3s = use_piece(bi, 3)
            wB0 = W2[:, 0:8192].rearrange("p (k c) -> p k c", k=8)
            wB1 = W3[:, 0:8 * 768].rearrange("p (k c) -> p k c", k=8)

            def wB(k, jb):
                if jb < 8:
                    return wB0[:, k, jb * 128:(jb + 1) * 128]
                return wB1[:, k, (jb - 8) * 128:(jb - 7) * 128]

            P2STOP = float(_os.environ.get('P2STOP', '99'))
            rgroups = [(c, min(c + GN, npc), 0) for c in range(0, npc, GN)]
            if has_s:
                rgroups = [(HC0, HC0 + 16, 2)] + rgroups + [(SC0, SC0 + 64, 1)]
            for (c0, c1, kind) in rgroups:
                n = c1 - c0
                for jb in range(14):
                    pa = pst[jb % 2]
                    for k in range(8):
                        mm(pa[:, 0:n], wB(k, jb), hT[:, k, c0:c1], k == 0, k == 7, ["hT", W2s, W3s], [f"ps{jb%2}"])
                    if kind == 2:
                        cp(ush[:, jb, :], pa[:, 0:16], [f"ps{jb%2}"], ["ush"], eng="act")
                        continue
                    u = ubj[jb % 2]
                    us = f"ubj{jb%2}"
                    cp(u[:, 1:n + 1], pa[:, 0:n], [f"ps{jb%2}"], [us], eng="act")
                    if kind == 0:
                        cp(u[:, 0:1], carry[:, jb:jb + 1], ["carry", us], [us])
                        cp(carry[:, jb:jb + 1], u[:, n:n + 1], [us, "carry"], ["carry"])
                        up_ap = u[:, 0:n]
                        ups = us
                    else:
                        cp(upj[:, 0:64].rearrange("p (b t) -> p b t", t=4)[:, :, 1:4], u[:, 1:65].rearrange("p (b t) -> p b t", t=4)[:, :, 0:3], [us], ["upj"])
                        cp(upj[:, 0:64].rearrange("p (b t) -> p b t", t=4)[:, :, 0], ush[:, jb, :], ["ush", "upj"], ["upj"])
                        up_ap = upj[:, 0:64]
                        ups = "upj"
                    tt(dtmp[:, 0:n], up_ap, u[:, 1:n + 1], ALU.subtract, [us, ups], ["dtmp"])
                    if jb < 4:
                        dst, ds = xr[:, jb, 0:n], "xr"
                    elif jb < 8:
                        dst, ds = xk[:, jb - 4, 0:n], "xk"
                    elif jb < 12:
                        dst, ds = xv[:, jb - 8, 0:n], "xv"
                    elif jb == 12:
                        dst, ds = xwa[:, 0:n], "xwa"
                    else:
                        dst, ds = xg[:, 0:n], "xg"
                    stt(dst, dtmp[:, 0:n], pv[:, PV_MU + jb:PV_MU + jb + 1], u[:, 1:n + 1], ALU.mult, ALU.add, ["dtmp", us, "pv"], [ds])
                if kind == 2:
                    continue
                if P2STOP < 1: continue
                is_s = kind == 1
                csz = 4 if is_s else 64
                nch = n // csz
                cmask = cstf[:, 512:576] if is_s else cstf[:, 0:n]
                act(twa[0:64, 0:n], xwa[0:64, 0:n], AF.Tanh, ["xwa"], ["twa"])
                cp(twa[64:128, 0:n], xwa[64:128, 0:n], ["xwa", "twa"], ["twa"])
                act(sgd[:, 0:n], xg[:, 0:n], AF.Sigmoid, ["xg"], ["sgd"])
                for j in range(4):
                    mm(pst[0][:, 0:n], wlora[0:64, j * 128:(j + 1) * 128], twa[0:64, 0:n], True, True, ["wlora", "twa"], ["ps0"])
                    mm(pst[1][:, 0:n], wlora[64:128, j * 128:(j + 1) * 128], twa[64:128, 0:n], True, True, ["wlora", "twa"], ["ps1"])
                    act(sgw[:, 0:n], pst[0][:, 0:n], AF.Sigmoid, ["ps0", "pv"], ["sgw"], bias=pv[:, PV_W0 + j:PV_W0 + j + 1])
                    act(aa[:, 0:n], pst[1][:, 0:n], AF.Sigmoid, ["ps1", "pv"], ["aa"], bias=pv[:, PV_A0 + j:PV_A0 + j + 1])
                    P.op("dve", lambda e, n=n, cmask=cmask: e.tensor_tensor_scan(cs_[:, 0:n], cmask, sgw[:, 0:n], 0.0, ALU.mult, ALU.add),
                         ["sgw", "cstf"], ["cs"])
                    act(Gt[:, 0:n], cs_[:, 0:n], AF.Exp, ["cs"], ["Gt"], scale=-DECAY_SCALE)
                    act(Gi[:, 0:n], cs_[:, 0:n], AF.Exp, ["cs"], ["Gi"], scale=DECAY_SCALE)
                    tt(cs_[:, 0:n], cs_[:, 0:n], sgw[:, 0:n], ALU.subtract, ["cs", "sgw", "Gt", "Gi"], ["cs"])
                    act(Gex[:, 0:n], cs_[:, 0:n], AF.Exp, ["cs"], ["Gex"], scale=-DECAY_SCALE)
                    ts(kkf[:, 0:n], xk[:, j, 0:n], pv[:, PV_KK + j:PV_KK + j + 1], None, ALU.mult, None, ["xk", "pv"], ["kkf"])
                    act(kk2[:, 0:n], kkf[:, 0:n], AF.Square, ["kkf"], ["kk2"])
                    mm(pst[2][:, 0:n], cst[:, C_ONESBLK:C_ONESBLK + 128], kk2[:, 0:n], True, True, ["cst", "kk2"], ["ps2"])
                    ts(rn[:, 0:n], pst[2][:, 0:n], 1e-24, None, ALU.max, None, ["ps2"], ["rn"])
                    act(rn[:, 0:n], rn[:, 0:n], AF.Ln, ["rn"], ["rn"])
                    act(rn[:, 0:n], rn[:, 0:n], AF.Exp, ["rn"], ["rn"], scale=-0.5)
                    tt(kkn[:, 0:n], kkf[:, 0:n], rn[:, 0:n], ALU.mult, ["kkf", "rn"], ["kkn"])
                    ts(t1[:, 0:n], aa[:, 0:n], -1.0, None, ALU.add, None, ["aa"], ["t1"])
                    ts(t1[:, 0:n], t1[:, 0:n], pv[:, PV_KA + j:PV_KA + j + 1], None, ALU.mult, None, ["t1", "pv"], ["t1"])
                    stt(km[:, 0:n], t1[:, 0:n], 1.0, xk[:, j, 0:n], ALU.add, ALU.mult, ["t1", "xk"], ["km"])
                    stt(pT[:, j, 0:n], xr[:, j, 0:n], pv[:, PV_RK + j:PV_RK + j + 1], km[:, 0:n], ALU.mult, ALU.mult, ["xr", "pv", "km"], ["pT"])
                    tt(atT_m[0][0:64, j, 0:n], kkn[0:64, 0:n], Gex[0:64, 0:n], ALU.mult, ["kkn", "Gex"], ["atT"])
                    tt(atT_m[1][64:128, j, 0:n], kkn[64:128, 0:n], Gex[64:128, 0:n], ALU.mult, ["kkn", "Gex", "atT"], ["atT"])
                    tt(ka_[:, 0:n], kkn[:, 0:n], aa[:, 0:n], ALU.mult, ["kkn", "aa"], ["ka_"])
                    stt(btT[:, j, 0:n], ka_[:, 0:n], -1.0, Gi[:, 0:n], ALU.mult, ALU.mult, ["ka_", "Gi"], ["btT"])
                    tt(kt2[:, j, 0:n], km[:, 0:n], Gi[:, 0:n], ALU.mult, ["km", "Gi"], ["kt2"])
                    tt(rtT_m[0][0:64, j, 0:n], xr[0:64, j, 0:n], Gt[0:64, 0:n], ALU.mult, ["xr", "Gt"], ["rtT"])
                    tt(rtT_m[1][64:128, j, 0:n], xr[64:128, j, 0:n], Gt[64:128, 0:n], ALU.mult, ["xr", "Gt", "rtT"], ["rtT"])
                    Gv = Gt[:, 0:n].rearrange("p (c t) -> p c t", t=csz)
                    gcb = Gv[:, :, csz - 1:csz].to_broadcast([128, nch, csz])
                    tt(bhT[:, j, 0:n].rearrange("p (c t) -> p c t", t=csz), btT[:, j, 0:n].rearrange("p (c t) -> p c t", t=csz), gcb, ALU.mult, ["btT", "Gt"], ["bhT"])
                    tt(khT2[:, j, 0:n].rearrange("p (c t) -> p c t", t=csz), kt2[:, j, 0:n].rearrange("p (c t) -> p c t", t=csz), gcb, ALU.mult, ["kt2", "Gt"], ["khT2"])
                    cp(gC[:, j, 0:nch], Gv[:, :, csz - 1], ["Gt"], ["gC"])
                    if is_s:
                        cp(GtT[:, j, :], Gt[:, 0:64], ["Gt"], ["GtT"])
                    cp(vT[:, j, 0:n], xv[:, j, 0:n], ["xv"], ["vT"], eng="act")
                if P2STOP < 2: continue
                rows = n
                R0 = slice(0, rows)
                cc = slice(0, rows)
                for j in range(4):
                    tr(psb[0][R0, j * 128:(j + 1) * 128], vT[:, j, cc], ident, ["vT", "cst"], ["ps0"])
                    tr(psb[1][R0, j * 128:(j + 1) * 128], bhT[:, j, cc], ident, ["bhT", "cst"], ["ps1"])
                    tr(psb[2][R0, j * 128:(j + 1) * 128], khT2[:, j, cc], ident, ["khT2", "cst"], ["ps2"])
                cp(v2[R0, :], psb[0][R0, 0:512], ["ps0"], ["v2"], eng="act")
                cp(bh_c[0][0:64, :], psb[1][0:64, 0:512], ["ps1"], ["bh_tm"], eng="act")
                cp(kh_c[0][0:64, :], psb[2][0:64, 0:512], ["ps2"], ["kh2_tm"], eng="act")
                if rows == 128:
                    cp(bh_c[1][64:128, :], psb[1][64:128, 0:512], ["ps1", "bh_tm"], ["bh_tm"], eng="act")
                    cp(kh_c[1][64:128, :], psb[2][64:128, 0:512], ["ps2", "kh2_tm"], ["kh2_tm"], eng="act")
                bh_tm, kh2_tm = bh_c[0], kh_c[0]
                if P2STOP < 2.5: continue
                mm(pst[3][R0, :], sgd[:, cc], wgu[:, :], True, True, ["sgd", "wgu"], ["ps3"])
                cp(g_tm[R0, :], pst[3][R0, :], ["ps3"], ["g_tm"], eng="act")
                if P2STOP < 2.8: continue
                for j in range(4):
                    mm(pst[4][R0, 0:64], pT[:, j, cc], cst[:, C_HSEL4 + j * 64:C_HSEL4 + (j + 1) * 64], j == 0, j == 3, ["pT", "cst"], ["ps4"])
                cp(bsum[R0, :], pst[4][R0, 0:8], ["ps4"], ["bsum"])
                if P2STOP < 3: continue
                mST = cst[R0, C_SST:C_SST + 64] if is_s else cst[:, C_MST:C_MST + 128]
                mITm = cst[R0, C_SIT:C_SIT + 64] if is_s else cst[:, C_MIT:C_MIT + 128]
                mLm = cst[R0, C_SL:C_SL + 64] if is_s else cst[:, C_ML:C_ML + 128]
                bc4 = lambda m: m.rearrange("p (o c) -> p o c", o=1).to_broadcast([rows, 4, rows])

                def fma(bufm, h):
                    return bufm[h % 2][:, h // 2, cc]

                def fmf(buf, h):
                    return buf[:, h // 2, cc]
                p0 = pst[0][:].rearrange("p (h c) -> p h c", h=4)
                p1 = pst[1][:].rearrange("p (h c) -> p h c", h=4)
                p2 = pst[2][:].rearrange("p (h c) -> p h c", h=4)
                p3 = pst[3][:].rearrange("p (h c) -> p h c", h=4)
                p4 = pst[4][:].rearrange("p (h c) -> p h c", h=4)
                for hg in range(2):
                    heads = [hg + 2 * q for q in range(4)]
                    for q, h in enumerate(heads):
                        mm(p0[R0, q, 0:rows], fma(atT_m, h), fmf(btT, h), True, True, ["atT", "btT"], ["ps0"])
                        mm(p1[R0, q, 0:rows], fmf(btT, h), fma(atT_m, h), True, True, ["atT", "btT"], ["ps1"])
                        mm(p2[R0, q, 0:rows], fmf(kt2, h), fma(atT_m, h), True, True, ["atT", "kt2"], ["ps2"])
                        mm(p3[R0, q, 0:rows], fmf(btT, h), fma(rtT_m, h), True, True, ["rtT", "btT"], ["ps3"])
                        mm(p4[R0, q, 0:rows], fmf(kt2, h), fma(rtT_m, h), True, True, ["rtT", "kt2"], ["ps4"])
                    tt(Lb[0][R0, :, 0:rows], p0[R0, :, 0:rows], bc4(mLm), ALU.mult, ["ps0", "cst"], ["Lb0"])
                    tt(Mb[0][R0, :, 0:rows], p1[R0, :, 0:rows], bc4(mST), ALU.mult, ["ps1", "cst"], ["Mb0"])
                    tt(AakT[hg][R0, :, 0:rows], p2[R0, :, 0:rows], bc4(mST), ALU.mult, ["ps2", "cst"], [f"AakT{hg}"])
                    tt(ArbT[hg][R0, :, 0:rows], p3[R0, :, 0:rows], bc4(mITm), ALU.mult, ["ps3", "cst"], [f"ArbT{hg}"])
                    tt(ArkT[hg][R0, :, 0:rows], p4[R0, :, 0:rows], bc4(mITm), ALU.mult, ["ps4", "cst"], [f"ArkT{hg}"])
                    nlev = 1 if is_s else 5
                    cur = 0
                    Rcur, Rs = Rb[0], "Rb0"
                    tt(Rb[0][R0, :, 0:rows], Mb[0][R0, :, 0:rows], bc4(ident[R0, R0]), ALU.add, ["Mb0", "cst"], ["Rb0"])
                    for lev in range(nlev):
                        nx = 1 - cur
                        for q in range(4):
                            mm(p0[R0, q, 0:rows], Mb[cur][R0, q, 0:rows], Lb[cur][R0, q, 0:rows], True, True, [f"Lb{cur}", f"Mb{cur}"], ["ps0"])
                            mm(p1[R0, q, 0:rows], Lb[cur][R0, q, 0:rows], Mb[cur][R0, q, 0:rows], True, True, [f"Lb{cur}", f"Mb{cur}"], ["ps1"])
                        cp(Lb[nx][R0, :, 0:rows], p0[R0, :, 0:rows], ["ps0"], [f"Lb{nx}"], eng="act")
                        cp(Mb[nx][R0, :, 0:rows], p1[R0, :, 0:rows], ["ps1"], [f"Mb{nx}"], eng="act")
                        for q in range(4):
                            mm(p2[R0, q, 0:rows], Lb[nx][R0, q, 0:rows], Rcur[R0, q, 0:rows], True, True, [f"Lb{nx}", Rs], ["ps2"])
                        last = lev == nlev - 1
                        Rn, Rns = (TTb[hg], f"TTb{hg}") if last else (Rb[nx], f"Rb{nx}")
                        tt(Rn[R0, :, 0:rows], p2[R0, :, 0:rows], Rcur[R0, :, 0:rows], ALU.add, ["ps2", Rs], [Rns])
                        Rcur, Rs = Rn, Rns
                        cur = nx
                if P2STOP < 4: continue
                px = pst[5][:].rearrange("p (h c) -> p h c", h=8)
                pu = pst[6][:].rearrange("p (h c) -> p h c", h=8)
                py = pst[7][:].rearrange("p (h c) -> p h c", h=8)
                ph = pst[3][:, 0:256].rearrange("p (j c) -> p j c", j=4)
                hsl = lambda h: slice((h % 2) * 64, (h % 2) * 64 + 64)
                if not is_s:
                    for ci in range(2):
                        rs = slice(ci * 64, ci * 64 + 64)
                        hc = state["hrp"]
                        ccs = slice(ci * 64, ci * 64 + 64)
                        for h in range(8):
                            q, hg = h // 2, h % 2
                            mm(px[rs, h, :], atT_m[hg][:, h // 2, ccs], Hrb[hc][:, h // 2, :], True, False, ["atT", f"Hrb{hc}"], ["ps5"])
                            mm(px[rs, h, :], AakT[hg][:, q, rs], v2[:, h * 64:(h + 1) * 64], False, True, [f"AakT{hg}", "v2"], ["ps5"])
                        cp(Xb[rs], px[rs], ["ps5"], ["Xb"], eng="act")
                        for h in range(8):
                            q, hg = h // 2, h % 2
                            mm(pu[rs, h, :], TTb[hg][:, q, rs], Xb[:, h, :], True, True, [f"TTb{hg}", "Xb"], ["ps6"])
                        cp(Ub[rs], pu[rs], ["ps6"], ["Ub"], eng="act")
                        for h in range(8):
                            q, hg = h // 2, h % 2
                            mm(py[rs, h, :], rtT_m[hg][:, h // 2, ccs], Hrb[hc][:, h // 2, :], True, False, ["rtT", f"Hrb{hc}"], ["ps7"])
                            mm(py[rs, h, :], ArbT[hg][:, q, rs], Ub[:, h, :], False, False, [f"ArbT{hg}", "Ub"], ["ps7"])
                            mm(py[rs, h, :], ArkT[hg][:, q, rs], v2[:, h * 64:(h + 1) * 64], False, True, [f"ArkT{hg}", "v2"], ["ps7"])
                            mm(ph[hsl(h), h // 2, :], bh_c[ci][:, h * 64:(h + 1) * 64], Ub[:, h, :], True, False, ["bh_tm", "Ub"], ["ps3"])
                            mm(ph[hsl(h), h // 2, :], kh_c[ci][:, h * 64:(h + 1) * 64], v2[:, h * 64:(h + 1) * 64], False, True, ["kh2_tm", "v2"], ["ps3"])
                        for jj in range(4):
                            stt(Hr[:, jj, :], Hr[:, jj, :], gC[:, jj, ci:ci + 1], ph[:, jj, :], ALU.mult, ALU.add, ["Hr", "gC", "ps3"], ["Hr"])
                        nxt = 1 - hc
                        cp(Hrb[nxt][:], Hr[:], ["Hr"], [f"Hrb{nxt}"], eng="act")
                        state["hrp"] = nxt
                else:
                    selq = lambda sl: cst[sl, C_SELQ:C_SELQ + 1024].rearrange("p (b t) -> p b t", b=16)
                    for h in range(8):
                        q, hg = h // 2, h % 2
                        pb_ = hsl(h)
                        jj = h // 2
                        P.dma("sp", s0n[0:64], swk[:, h, :, :].rearrange("b i k -> i b k"), writes=["s0n"], key="s0n")
                        cp(s0c[0:64], s0n[0:64], ["s0n"], ["s0c"], eng="act")
                        pt = psb[4][:].rearrange("p (b i) -> p b i", b=16)
                        for b in range(16):
                            tr(pt[pb_, b, :], s0c[0:64, b, :], ident[0:64, 0:64], ["s0c", "cst"], ["ps4"])
                        cp(hsb[pb_, jj], pt[pb_], ["ps4"], ["hsb"], eng="act")
                        tt(atm[:], atT_m[hg][:, jj, 0:64].rearrange("p (o t) -> p o t", o=1).to_broadcast([128, 16, 64]), selq(slice(0, 128)), ALU.mult, ["atT", "cst"], ["atm"])
                        for b in range(16):
                            mm(px[0:64, h, :], atm[:, b, :], hsb[:, jj, b, :], b == 0, False, ["atm", "hsb"], ["ps5"])
                        mm(px[0:64, h, :], AakT[hg][0:64, q, 0:64], v2[0:64, h * 64:(h + 1) * 64], False, True, [f"AakT{hg}", "v2"], ["ps5"])
                    cp(Xb[0:64], px[0:64], ["ps5"], ["Xb"], eng="act")
                    for h in range(8):
                        q, hg = h // 2, h % 2
                        mm(pu[0:64, h, :], TTb[hg][0:64, q, 0:64], Xb[0:64, h, :], True, True, [f"TTb{hg}", "Xb"], ["ps6"])
                    cp(Ub[0:64], pu[0:64], ["ps6"], ["Ub"], eng="act")
                    for j in range(4):
                        tr(pst[3][0:64, j * 128:(j + 1) * 128], GtT[:, j, :], identf, ["GtT", "cstf"], ["ps3"])
                    cp(G_tm[0:64, :], pst[3][0:64, :], ["ps3"], ["G_tm"], eng="act")
                    selv = cst[0:64, C_SELV:C_SELV + 16].rearrange("p (b o) -> p b o", o=1).to_broadcast([64, 16, 64])
                    esel = cstf[0:64, 768:1792].rearrange("p (b i) -> p b i", b=16)
                    for h in range(8):
                        q, hg = h // 2, h % 2
                        pb_ = hsl(h)
                        jj = h // 2
                        tt(rtm[:], rtT_m[hg][:, jj, 0:64].rearrange("p (o t) -> p o t", o=1).to_broadcast([128, 16, 64]), selq(slice(0, 128)), ALU.mult, ["rtT", "cst"], ["rtm"])
                        for b in range(16):
                            mm(py[0:64, h, :], rtm[:, b, :], hsb[:, jj, b, :], b == 0, False, ["rtm", "hsb"], ["ps7"])
                        mm(py[0:64, h, :], ArbT[hg][0:64, q, 0:64], Ub[0:64, h, :], False, False, [f"ArbT{hg}", "Ub"], ["ps7"])
                        mm(py[0:64, h, :], ArkT[hg][0:64, q, 0:64], v2[0:64, h * 64:(h + 1) * 64], False, True, [f"ArkT{hg}", "v2"], ["ps7"])
                        P.dma("sp", s0n[0:64], swk[:, h, :, :].rearrange("b i k -> i b k"), writes=["s0n"], key="s0n")
                        tt(bdb[0:64], bh_tm[0:64, h * 64:(h + 1) * 64].rearrange("p (o k) -> p o k", o=1).to_broadcast([64, 16, 64]), selv, ALU.mult, ["bh_tm", "cst"], ["bdb"])
                        tt(bdk[0:64], kh2_tm[0:64, h * 64:(h + 1) * 64].rearrange("p (o k) -> p o k", o=1).to_broadcast([64, 16, 64]), selv, ALU.mult, ["kh2_tm", "cst"], ["bdk"])
                        for half in range(2):
                            bs = slice(half * 8, half * 8 + 8)
                            pg = pst[0 + half]
                            for b in range(8):
                                mm(pg[0:64, b * 64:(b + 1) * 64], esel[:, half * 8 + b, :], G_tm[0:64, h * 64:(h + 1) * 64], True, True, ["cstf", "G_tm"], [f"ps{half}"])
                            pq_ = pst[2] if half == 0 else pst[4]
                            pqs = "ps2" if half == 0 else "ps4"
                            mm(pq_[0:64, :], Ub[0:64, h, :], bdb[0:64, bs, :], True, False, ["Ub", "bdb"], [pqs])
                            mm(pq_[0:64, :], v2[0:64, h * 64:(h + 1) * 64], bdk[0:64, bs, :], False, True, ["v2", "bdk"], [pqs])
                            sv = s0n[0:64, bs, :]
                            tt(sv, sv, pg[0:64, :].rearrange("p (b k) -> p b k", b=8), ALU.mult, ["s0n", f"ps{half}"], ["s0n"])
                            tt(sv, sv, pq_[0:64, :].rearrange("p (b k) -> p b k", b=8), ALU.add, ["s0n", pqs], ["s0n"])
                        P.dma("sp", wk_s[:, h, :, :].rearrange("b i k -> i b k"), s0n[0:64], reads=["s0n"], key="o_wks", final=True)
                if P2STOP < 5: continue
                cp(y_tm[R0, :], pst[7][R0, :], ["ps7"], ["y_tm"], eng="act")
                y3 = y_tm[R0, :].rearrange("p (h c) -> p h c", h=8)
                P.op("dve", lambda e, y3=y3, R0=R0: e.reduce_sum(st8[R0, 0:8], y3, AX.X), ["y_tm"], ["st8a"])
                act(ysq[R0, :], y_tm[R0, :], AF.Square, ["y_tm"], ["ysq"])
                ys3 = ysq[R0, :].rearrange("p (h c) -> p h c", h=8)
                P.op("dve", lambda e, ys3=ys3, R0=R0: e.reduce_sum(st8[R0, 8:16], ys3, AX.X), ["ysq"], ["st8b"])
                ts(st8[R0, 16:24], st8[R0, 0:8], 1.0 / 64, None, ALU.mult, None, ["st8a"], ["st8m"])
                tt(st8[R0, 24:32], st8[R0, 16:24], st8[R0, 16:24], ALU.mult, ["st8m"], ["st8v"])
                stt(st8[R0, 24:32], st8[R0, 8:16], 1.0 / 64, st8[R0, 24:32], ALU.mult, ALU.subtract, ["st8b", "st8v"], ["st8v"])
                rsqrt_small(st8[R0, 32:40], st8[R0, 24:32], 1.0, GN_EPS, ["st8v"], ["st8r"])
                bc8 = lambda a: a.rearrange("p (h o) -> p h o", o=1).to_broadcast([rows, 8, 64])
                tt(y3, y3, bc8(st8[R0, 16:24]), ALU.subtract, ["y_tm", "st8m", "ysq"], ["y_tm"])
                tt(y3, y3, bc8(st8[R0, 32:40]), ALU.mult, ["y_tm", "st8r"], ["y_tm"])
                tt(y_tm[R0, :], y_tm[R0, :], pbc[R0, PB_LXW:PB_LXW + 512], ALU.mult, ["y_tm", "pbc"], ["y_tm"])
                tt(y_tm[R0, :], y_tm[R0, :], pbc[R0, PB_LXB:PB_LXB + 512], ALU.add, ["y_tm", "pbc"], ["y_tm"])
                tt(ysq[R0, :].rearrange("p (h c) -> p h c", h=8), v2[R0, :].rearrange("p (h c) -> p h c", h=8), bc8(bsum[R0, :]), ALU.mult,
                   ["v2", "bsum", "ysq"], ["ysq"])
                tt(y_tm[R0, :], y_tm[R0, :], ysq[R0, :], ALU.add, ["y_tm", "ysq"], ["y_tm"])
                tt(ob_tm[R0, :], y_tm[R0, :], g_tm[R0, :], ALU.mult, ["y_tm", "g_tm"], ["ob_tm"])
                for j in range(4):
                    tr(psb[0][:, j * 128:j * 128 + rows], ob_tm[R0, j * 128:(j + 1) * 128], ident[R0, R0], ["ob_tm", "cst"], ["ps0"])
                cp(oT[:, 4:8, c0:c0 + rows], psb[0][:, 0:512].rearrange("p (j c) -> p j c", j=4)[:, :, 0:rows], ["ps0"], ["oT"], eng="act")
            done_piece(bi, 2)
            done_piece(bi, 3)
            import os as _os
            if bi == len(blocks) - 1 and not _os.environ.get('SKIP_HRT'):
                for j in range(4):
                    tr(pst[0][0:64, j * 128:(j + 1) * 128], Hr[:, j, :], identf, ["Hr", "cstf"], ["ps0"])
                cp(y_tm[0:64, :], pst[0][0:64, :], ["ps0"], ["y_tm"], eng="act")
                P.dma("sp", wk_p.rearrange("h i k -> i h k"), y_tm[0:64, :].rearrange("p (h k) -> p h k", h=8), reads=["y_tm"], key="o_wkp", final=True)

            DBG.update({k_: v_ for k_, v_ in locals().items() if not k_.startswith('_')})
            if stop_after == 'P2':
                break
            P.barrier()
            A = Arena()
            ntt = len(ttiles)
            xres = A.f32(ntt * 1024).rearrange("p (t d) -> p t d", t=ntt)
            mT = A.bf(8 * 512).rearrange("p (k c) -> p k c", k=8)
            sga = A.f32(512); m1 = A.f32(512); m2 = A.f32(512)
            hb2 = A.bf(1024); junk2 = A.bf(1024)
            aT = A.bf(4 * 512).rearrange("p (k c) -> p k c", k=4)
            rl = A.f32(512); yo = A.f32(1024)
            for i, (t, c0, rows) in enumerate(ttiles):
                src = xp[t * 128:(t + 1) * 128, :] if t < 16 else xs
                P.dma("sp", xres[0:rows, i, :], src, writes=[f"xres{i}"], key=f"xres{i}")
            W4, W4s = use_piece(bi, 4)
            W5, W5s = use_piece(bi, 5)
            W6, W6s = use_piece(bi, 6)
            W7, W7s = use_piece(bi, 7)
            wGa = W4[:, 0:8192].rearrange("p (k c) -> p k c", k=8)
            wGb = W5[:, 0:8192].rearrange("p (k c) -> p k c", k=8)
            wao = W6[:, 0:4096].rearrange("p (k c) -> p k c", k=4)
            wbo = W6[:, 4096:8192].rearrange("p (k c) -> p k c", k=4)
            wo = W7[:, 0:8192].rearrange("p (k c) -> p k c", k=8)
            ncs = npc + (64 if has_s else 0)
            g3 = [(c, min(c + 512, ncs)) for c in range(0, ncs, 512)]
            for (c0, c1) in g3:
                n = c1 - c0
                for dj in range(8):
                    dsl = slice(dj * 128, (dj + 1) * 128)
                    for k in range(4):
                        mm(pst[0][:, 0:n], wao[:, k, dsl], oT[:, k, c0:c1], k == 0, k == 3, ["oT", W6s], ["ps0"])
                    for k in range(8):
                        mm(pst[1][:, 0:n], wGa[:, k, dsl], hT[:, k, c0:c1], k == 0, k == 7, ["hT", W4s], ["ps1"])
                    act(sga[:, 0:n], pst[1][:, 0:n], AF.Sigmoid, ["ps1"], ["sga"])
                    tt(m1[:, 0:n], sga[:, 0:n], pst[0][:, 0:n], ALU.mult, ["sga", "ps0"], ["m1"])
                    for k in range(4):
                        mm(pst[2][:, 0:n], wbo[:, k, dsl], oT[:, 4 + k, c0:c1], k == 0, k == 3, ["oT", W6s], ["ps2"])
                    for k in range(8):
                        mm(pst[3][:, 0:n], wGb[:, k, dsl], hT[:, k, c0:c1], k == 0, k == 7, ["hT", W5s], ["ps3"])
                    act(sga[:, 0:n], pst[3][:, 0:n], AF.Sigmoid, ["ps3", "m1"], ["sga"])
                    tt(m2[:, 0:n], sga[:, 0:n], pst[2][:, 0:n], ALU.mult, ["sga", "ps2"], ["m2"])
                    tt(mT[:, dj, 0:n], m1[:, 0:n], m2[:, 0:n], ALU.add, ["m1", "m2"], ["mT"])
                for i, (t, tc0, rows) in enumerate(ttiles):
                    if not (c0 <= tc0 < c1):
                        continue
                    R0 = slice(0, rows)
                    lc = slice(tc0 - c0, tc0 - c0 + rows)
                    for half in range(2):
                        pp = pst[4 + half]
                        for k in range(8):
                            mm(pp[R0, :], mT[:, k, lc], wo[:, k, half * 512:(half + 1) * 512], k == 0, k == 7, ["mT", W7s], [f"ps{4+half}"])
                        tt(xres[R0, i, half * 512:(half + 1) * 512], xres[R0, i, half * 512:(half + 1) * 512], pp[R0, :], ALU.add, [f"xres{i}", f"ps{4+half}"], [f"xres{i}"])
                    act(junk2[R0, :], xres[R0, i, :], AF.Square, [f"xres{i}"], ["junk2", "ssq"], accum_out=small[R0, 0:1])
                    rsqrt_small(small[R0, 1:2], small[R0, 0:1], 1.0 / D, NORM_EPS, ["ssq"], ["rstd"])
                    stt(hb2[R0, :], xres[R0, i, :], small[R0, 1:2], pbc[R0, PB_GMLP:PB_GMLP + D], ALU.mult, ALU.mult, [f"xres{i}", "rstd", "pbc"], ["hb2"])
                    for k in range(8):
                        tr(psb[6][:, k * 128:k * 128 + rows], hb2[R0, k * 128:(k + 1) * 128], ident[R0, R0], ["hb2", "cst"], ["ps6"])
                    cp(hT[:, :, tc0:tc0 + rows], psb[6].rearrange("p (k c) -> p k c", k=8)[:, :, 0:rows], ["ps6"], ["hT"], eng="act")
            for _p in (4, 5, 6, 7):
                done_piece(bi, _p)
            if stop_after == 'P3':
                break
            for e8 in range(8):
                W8, W8s = use_piece(bi, 8 + e8)
                wu = W8[:, 0:4096].rearrange("p (k c) -> p k c", k=8)
                wd = W8[:, 4096:8192].rearrange("p (k c) -> p k c", k=4)
                for (c0, c1) in g3:
                    n = c1 - c0
                    for fb in range(4):
                        pa = pst[fb % 2]
                        for k in range(8):
                            mm(pa[:, 0:n], wu[:, k, fb * 128:(fb + 1) * 128], hT[:, k, c0:c1], k == 0, k == 7, ["hT", W8s], [f"ps{fb%2}"])
                        act(rl[:, 0:n], pa[:, 0:n], AF.Relu, [f"ps{fb%2}"], ["rl"])
                        tt(aT[:, fb, 0:n], rl[:, 0:n], rl[:, 0:n], ALU.mult, ["rl"], ["aT"])
                    for i, (t, tc0, rows) in enumerate(ttiles):
                        if not (c0 <= tc0 < c1):
                            continue
                        R0 = slice(0, rows)
                        lc = slice(tc0 - c0, tc0 - c0 + rows)
                        for half in range(2):
                            pp = pst[4 + half]
                            for k in range(4):
                                mm(pp[R0, :], aT[:, k, lc], wd[:, k, half * 512:(half + 1) * 512], k == 0, k == 3, ["aT", W8s], [f"ps{4+half}"])
                            tt(xres[R0, i, half * 512:(half + 1) * 512], xres[R0, i, half * 512:(half + 1) * 512], pp[R0, :], ALU.add, [f"xres{i}", f"ps{4+half}"], [f"xres{i}"])
                done_piece(bi, 8 + e8)
            if stop_after == 'P4':
                break
            for i, (t, tc0, rows) in enumerate(ttiles):
                R0 = slice(0, rows)
                act(junk2[R0, :], xres[R0, i, :], AF.Square, [f"xres{i}"], ["junk2", "ssq"], accum_out=small[R0, 0:1])
                rsqrt_small(small[R0, 1:2], small[R0, 0:1], 1.0 / D, NORM_EPS, ["ssq"], ["rstd"])
                stt(yo[R0, :], xres[R0, i, :], small[R0, 1:2], pbc[R0, PB_GFIN:PB_GFIN + D], ALU.mult, ALU.mult, [f"xres{i}", "rstd", "pbc"], ["yo"])
                dst = y_p[t * 128:(t + 1) * 128, :] if t < 16 else y_s
                P.dma("sp", dst, yo[R0, :], reads=["yo"], key="o_y", final=True)
        P.emit()
    return nc


_CACHE = {}


def kernel(x_prompt, x_sample, state_hgrn, state_wkv, state_shift, norm_mix_g, w_in, mu_shift,
           w_decay0, w_decay_up, a0, w_aaa_up, w_gate_up, k_k, k_a, r_k, ln_x_w, ln_x_b,
           lb_logits, hgrn_norm_w, w_a_out, w_b_out, w_out, norm_mlp_g, w_up, w_down, norm_final_g):
    f = lambda a: np.ascontiguousarray(np.asarray(a, dtype=np.float32))
    blocks = [[0, 1, 2, 3], [4, 5, 6, 7], [8, 9, 10, 11], [12, 13, 14, 15, 16]]
    if "nc" not in _CACHE:
        _CACHE["nc"] = build_nc(blocks)
    nc = _CACHE["nc"]
    cst, cstf = _build_consts()
    fm = lambda v, nb: f(v).reshape(nb, 128).T
    pv = np.zeros((128, NPV), np.float32)
    lbl = f(lb_logits)
    pv[:, PV_L0:PV_L0 + 4] = fm(lbl[0], 4)
    pv[:, PV_L1:PV_L1 + 4] = fm(lbl[1], 4)
    pv[:, PV_MU:PV_MU + 14] = fm(f(mu_shift)[0], 14)
    pv[:, PV_W0:PV_W0 + 4] = fm(f(w_decay0)[0], 4)
    pv[:, PV_A0:PV_A0 + 4] = fm(f(a0)[0], 4)
    pv[:, PV_KK:PV_KK + 4] = fm(f(k_k)[0], 4)
    pv[:, PV_KA:PV_KA + 4] = fm(f(k_a)[0], 4)
    pv[:, PV_RK:PV_RK + 4] = fm(f(r_k)[0].reshape(-1), 4)
    pb = np.zeros((128, NPB), np.float32)
    pb[:, PB_GMIX:PB_GMIX + D] = f(norm_mix_g)[0][None, :]
    pb[:, PB_GMLP:PB_GMLP + D] = f(norm_mlp_g)[0][None, :]
    pb[:, PB_GFIN:PB_GFIN + D] = f(norm_final_g)[None, :]
    pb[:, PB_HNW:PB_HNW + 512] = f(hgrn_norm_w)[0][None, :]
    pb[:, PB_LXW:PB_LXW + 512] = f(ln_x_w)[0][None, :]
    pb[:, PB_LXB:PB_LXB + 512] = f(ln_x_b)[0][None, :]
    wlora = np.concatenate([f(w_decay_up)[0], f(w_aaa_up)[0]], axis=0)
    shared = {"w_in": f(w_in)[0], "w_a_out": f(w_a_out)[0], "w_b_out": f(w_b_out)[0], "w_out": f(w_out)[0],
              "w_up": f(w_up)[0], "w_down": f(w_down)[0], "wlora": f(wlora), "wgu": f(w_gate_up)[0],
              "pv": pv, "pb": pb, "cst": cst, "cstf": cstf}
    xpn, xsn = f(x_prompt), f(x_sample)
    shn, wkn, sfn = f(state_hgrn)[0], f(state_wkv)[0], f(state_shift)[0]
    in_maps = []
    for c in range(NCORE):
        m = dict(shared)
        m["xp"] = xpn[c]
        m["xs"] = xsn[c * NSB:(c + 1) * NSB].reshape(64, D)
        m["shs"] = sfn[c * NSB:(c + 1) * NSB]
        m["shg"] = shn[c * NSB:(c + 1) * NSB]
        m["swk"] = wkn[c * NSB:(c + 1) * NSB]
        in_maps.append(m)
    res = run_bass_kernel_spmd(nc, in_maps, core_ids=list(range(NCORE)))
    R = res.results
    y_prompt = np.stack([R[c]["y_p"] for c in range(NCORE)], 0)
    y_sample = np.concatenate([R[c]["y_s"].reshape(NSB, 4, D) for c in range(NCORE)], 0)
    hgp = np.stack([R[c]["hg_p"] for c in range(NCORE)], 0)[None]
    wkp = np.stack([R[c]["wk_p"] for c in range(NCORE)], 0)[None]
    shp = np.concatenate([R[c]["sh_p"] for c in range(NCORE)], 0)[None]
    hgs = np.concatenate([R[c]["hg_s"] for c in range(NCORE)], 0)[None]
    wks = np.concatenate([R[c]["wk_s"] for c in range(NCORE)], 0)[None]
    shs_o = np.concatenate([R[c]["sh_s"] for c in range(NCORE)], 0)[None]
    return (y_prompt.astype(np.float32), y_sample.astype(np.float32), hgp.astype(np.float32), wkp.astype(np.float32),
            shp.astype(np.float32), hgs.astype(np.float32), wks.astype(np.float32), shs_o.astype(np.float32))
```

```python
import contextlib
import numpy as np
import concourse.bass as bass
import concourse.mybir as mybir
from concourse.bass_utils import run_bass_kernel_spmd

F32 = mybir.dt.float32
BF16 = mybir.dt.bfloat16
AF = mybir.ActivationFunctionType
ALU = mybir.AluOpType
AX = mybir.AxisListType

D = 1024
NCORE = 8
SEQ = 2048
NSB = 16
DECAY_SCALE = 0.6065306597126334
NORM_EPS = 1e-6
HGRN_EPS = 1e-5
GN_EPS = 64e-5


class _Instr:
    __slots__ = ("eng", "fn", "deps", "dma_key", "dma_val", "signal", "sem_val")

    def __init__(self, eng, fn):
        self.eng = eng
        self.fn = fn
        self.deps = []
        self.dma_key = None
        self.dma_val = 0
        self.signal = False
        self.sem_val = 0


class Prog:
    ENGS = ("pe", "act", "dve", "pool", "sp")

    def __init__(self, nc):
        self.nc = nc
        self.streams = {e: [] for e in self.ENGS}
        self.last_writer = {}
        self.readers = {}
        self.dma_count = {}
        self.dma_last = {}
        self.final_waits = []

    def _deps(self, ins, reads, writes):
        deps = ins.deps
        for s in reads:
            w = self.last_writer.get(s)
            if w is not None:
                deps.append(w)
        for s in writes:
            w = self.last_writer.get(s)
            if w is not None:
                deps.append(w)
            deps.extend(self.readers.get(s, ()))
        for s in reads:
            self.readers.setdefault(s, []).append(ins)
        for s in writes:
            self.last_writer[s] = ins
            self.readers[s] = []

    def op(self, eng, fn, reads=(), writes=()):
        ins = _Instr(eng, fn)
        self._deps(ins, reads, writes)
        self.streams[eng].append(ins)
        return ins

    def dma(self, eng, out, in_, reads=(), writes=(), key=None, final=False, **kw):
        ins = _Instr(eng, lambda e: e.dma_start(out=out, in_=in_, **kw))
        ins.dma_key = key
        self.dma_count[key] = self.dma_count.get(key, 0) + 1
        ins.dma_val = 16 * self.dma_count[key]
        self.dma_last[key] = ins
        self._deps(ins, reads, writes)
        self.streams[eng].append(ins)
        if final:
            self.final_waits.append(key)
        return ins

    def barrier(self):
        lasts = []
        for e in ("pe", "act", "dve", "pool"):
            for i in reversed(self.streams[e]):
                if i.dma_key is None and i.fn is not None:
                    lasts.append(i)
                    break
        dm = list(self.dma_last.values())
        for e in ("pe", "act", "dve", "pool", "sp"):
            ins = _Instr(e, None)
            ins.deps = list(lasts) + dm
            self.streams[e].append(ins)
        self.last_writer = {}
        self.readers = {}

    def emit(self):
        nc = self.nc
        for e in self.ENGS:
            for ins in self.streams[e]:
                for d in ins.deps:
                    if d.dma_key is None and d.fn is not None and not (d.eng == "pe" and e == "pe"):
                        d.signal = True
        for e in self.ENGS:
            c = 0
            for ins in self.streams[e]:
                if ins.dma_key is None and ins.signal:
                    c += 1
                    ins.sem_val = c
        with contextlib.ExitStack() as st:
            esem = {e: st.enter_context(nc.semaphore(f"s_{e}")) for e in ("pe", "act", "dve", "pool")}
            dsem = {k: st.enter_context(nc.semaphore(f"d_{i}")) for i, k in enumerate(self.dma_count)}
            block = st.enter_context(nc.Block())

            def run(ename):
                def body(eng):
                    waited = {}
                    for ins in self.streams[ename]:
                        need = {}
                        for d in ins.deps:
                            if d.dma_key is not None:
                                sem, val, k = dsem[d.dma_key], d.dma_val, ("d", d.dma_key)
                            else:
                                if d.eng == ename and ename == "pe":
                                    continue
                                sem, val, k = esem[d.eng], d.sem_val, ("e", d.eng)
                            if waited.get(k, 0) >= val:
                                continue
                            if k not in need or need[k][1] < val:
                                need[k] = (sem, val)
                        for k, (sem, val) in need.items():
                            eng.wait_ge(sem, val)
                            waited[k] = val
                        if ins.fn is None:
                            continue
                        bi = ins.fn(eng)
                        if ins.dma_key is not None:
                            bi.then_inc(dsem[ins.dma_key], 16)
                        elif ins.signal:
                            bi.then_inc(esem[ename], 1)
                    if ename == "sp":
                        for key in self.final_waits:
                            eng.wait_ge(dsem[key], 16 * self.dma_count[key])
                return body

            block.sync(run("sp"))
            block.tensor(run("pe"))
            block.scalar(run("act"))
            block.vector(run("dve"))
            block.gpsimd(run("pool"))


C_ID = 0
C_MIT = 128
C_MST = 256
C_ML = 384
C_SIT = 512
C_SST = 640
C_SL = 768
C_ONESBLK = 896
C_HEADSEL = 1024
C_SELQ = 1026
C_SELV = 2050
C_ESEL = 2066
C_HSEL4 = 2066 + 1024
NCST = 2066 + 1024 + 256
PV_L0, PV_L1, PV_MU, PV_W0, PV_A0, PV_KK, PV_KA, PV_RK = 0, 4, 8, 22, 26, 30, 34, 38
NPV = 42
PB_GMIX, PB_GMLP, PB_GFIN, PB_HNW, PB_LXW, PB_LXB = 0, 1024, 2048, 3072, 3584, 4096
NPB = 4608


def _build_consts():
    c = np.zeros((128, NCST), np.float32)
    i = np.arange(128)
    c[:, C_ID:C_ID + 128] = np.eye(128)
    same = (i[:, None] // 64) == (i[None, :] // 64)
    c[:, C_MIT:C_MIT + 128] = same & (i[:, None] <= i[None, :])
    c[:, C_MST:C_MST + 128] = same & (i[:, None] < i[None, :])
    c[:, C_ML:C_ML + 128] = same & (i[None, :] < i[:, None])
    j = np.arange(64)
    s4 = (j[:, None] // 4) == (j[None, :] // 4)
    c[:64, C_SIT:C_SIT + 64] = s4 & (j[:, None] <= j[None, :])
    c[:64, C_SST:C_SST + 64] = s4 & (j[:, None] < j[None, :])
    c[:64, C_SL:C_SL + 64] = s4 & (j[None, :] < j[:, None])
    c[:, C_ONESBLK:C_ONESBLK + 128] = same
    c[:64, C_HEADSEL] = 1.0
    c[64:, C_HEADSEL + 1] = 1.0
    selq = (np.arange(16)[:, None] == (j[None, :] // 4)).astype(np.float32)
    c[:, C_SELQ:C_SELQ + 1024] = selq.reshape(1, 1024)
    c[:64, C_SELV:C_SELV + 16] = ((j[:, None] // 4) == np.arange(16)[None, :])
    esel = np.zeros((64, 16, 64), np.float32)
    for b in range(16):
        esel[4 * b + 3, b, :] = 1.0
    c[:64, C_ESEL:C_ESEL + 1024] = esel.reshape(64, 1024)
    for jj in range(4):
        c[:64, C_HSEL4 + jj * 64 + 2 * jj] = 1.0
        c[64:, C_HSEL4 + jj * 64 + 2 * jj + 1] = 1.0
    cf = np.zeros((128, 1792), np.float32)
    cm = np.ones(512, np.float32); cm[::64] = 0.0
    cf[:, 0:512] = cm
    cs = np.ones(64, np.float32); cs[::4] = 0.0
    cf[:, 512:576] = cs
    cf[:, 640:768] = np.eye(128)
    cf[:64, 768:1792] = esel.reshape(64, 1024)
    return c, cf


DBG = {}


def build_nc(blocks, last_pt=15, stop_after=None):
    nc = bass.Bass("TRN2", target_bir_lowering=False)
    din = lambda n, s: nc.dram_tensor(n, s, F32, kind="ExternalInput").ap()
    dout = lambda n, s: nc.dram_tensor(n, s, F32, kind="ExternalOutput").ap()
    xp = din("xp", [SEQ, D]); xs = din("xs", [64, D]); shs = din("shs", [NSB, D])
    shg = din("shg", [NSB, 4, 128, 128]); swk = din("swk", [NSB, 8, 64, 64])
    w_in = din("w_in", [D, 5888]); w_a_out = din("w_a_out", [512, D]); w_b_out = din("w_b_out", [512, D])
    w_out = din("w_out", [D, D]); w_up = din("w_up", [D, 4096]); w_down = din("w_down", [4096, D])
    wlora_d = din("wlora", [128, 512]); wgu_d = din("wgu", [128, 512])
    pv_d = din("pv", [128, NPV]); pb_d = din("pb", [128, NPB])
    cst_d = din("cst", [128, NCST]); cstf_d = din("cstf", [128, 1792])
    y_p = dout("y_p", [SEQ, D]); y_s = dout("y_s", [64, D])
    hg_p = dout("hg_p", [4, 128, 128]); wk_p = dout("wk_p", [8, 64, 64]); sh_p = dout("sh_p", [1, D])
    hg_s = dout("hg_s", [NSB, 4, 128, 128]); wk_s = dout("wk_s", [NSB, 8, 64, 64]); sh_s = dout("sh_s", [NSB, D])

    w_in_v = w_in.rearrange("(kt p) c -> p kt c", p=128)
    w_up_v = w_up.rearrange("(kt p) c -> p kt c", p=128)
    w_dn_v = w_down.rearrange("(kt p) c -> p kt c", p=128)
    w_ao_v = w_a_out.rearrange("(kt p) c -> p kt c", p=128)
    w_bo_v = w_b_out.rearrange("(kt p) c -> p kt c", p=128)
    w_o_v = w_out.rearrange("(kt p) c -> p kt c", p=128)

    MAXC = 592
    with contextlib.ExitStack() as st:
        sb = lambda n, s, d: st.enter_context(nc.sbuf_tensor("sb_" + n, s, d))
        P = Prog(nc)
        cst = sb("cst", [128, NCST], BF16)
        cstf = sb("cstf", [128, 1792], F32)
        pv = sb("pv", [128, NPV + 16], F32)
        pbc = sb("pbc", [128, NPB], F32)
        wlora = sb("wlora", [128, 512], BF16)
        wgu = sb("wgu", [128, 512], BF16)
        ring = [sb(f"ring{i}", [128, 8192], BF16) for i in range(4)]
        hT = sb("hT", [128, 8, MAXC], BF16)
        oT = sb("oT", [128, 8, MAXC], BF16)
        Sh = sb("Sh", [128, 4, 128], F32)
        Shb = [sb(f"Shb{i}", [128, 4, 128], BF16) for i in range(2)]
        Hr = sb("Hr", [128, 4, 64], F32)
        Hrb = [sb(f"Hrb{i}", [128, 4, 64], BF16) for i in range(2)]
        carry = sb("carry", [128, 16], F32)
        small = sb("small", [128, 64], F32)
        scr = sb("scr", [128, 21504], F32)
        pst = [st.enter_context(nc.psum_tensor(f"ps{i}", [128, 512], F32)) for i in range(8)]
        psb = [p[:].bitcast(BF16) for p in pst]

        ident = cst[:, C_ID:C_ID + 128]
        identf = cstf[:, 640:768]

        class Arena:
            def __init__(self):
                self.off = 0

            def f32(self, n):
                a = scr[:, self.off:self.off + n]
                self.off += n
                assert self.off <= 21504, self.off
                return a

            def bf(self, n):
                m = (n + 1) // 2
                a = scr[:, self.off:self.off + m].bitcast(BF16)
                self.off += m
                assert self.off <= 21504, self.off
                return a

        def mm(out, lhsT, rhs, start, stop, r, w):
            P.op("pe", lambda e: e.matmul(out, lhsT, rhs, start=start, stop=stop), r, w)

        def tr(out, in_, idn, r, w):
            P.op("pe", lambda e: e.transpose(out, in_, idn), r, w)

        def act(out, in_, func, r, w, bias=0.0, scale=1.0, accum_out=None):
            if accum_out is None:
                P.op("act", lambda e: e.activation(out, in_, func, bias=bias, scale=scale), r, w)
            else:
                P.op("act", lambda e: e.activation(out, in_, func, bias=bias, scale=scale, accum_out=accum_out), r, w)

        def tt(out, a, b, op, r, w, eng="dve"):
            P.op(eng, lambda e: e.tensor_tensor(out, a, b, op), r, w)

        def ts(out, a, s1, s2, op0, op1, r, w, eng="dve"):
            if s2 is None:
                P.op(eng, lambda e: e.tensor_scalar(out, a, s1, None, op0), r, w)
            else:
                P.op(eng, lambda e: e.tensor_scalar(out, a, s1, s2, op0, op1), r, w)

        def stt(out, a, s, b, op0, op1, r, w, eng="dve"):
            P.op(eng, lambda e: e.scalar_tensor_tensor(out, a, s, b, op0, op1), r, w)

        def scan(out, d0, d1, r, w):
            P.op("dve", lambda e: e.tensor_tensor_scan(out, d0, d1, 0.0, ALU.mult, ALU.add), r, w)

        def rsum(out, in_, r, w):
            P.op("dve", lambda e: e.reduce_sum(out, in_, AX.X), r, w)

        def cp(out, in_, r, w, eng="dve"):
            if eng == "act":
                P.op("act", lambda e: e.copy(out, in_), r, w)
            else:
                P.op(eng, lambda e: e.tensor_copy(out, in_), r, w)

        def rsqrt_small(out, in_, scale, eps, r, w):
            act(out, in_, AF.Ln, r, w, bias=eps, scale=scale)
            act(out, out, AF.Exp, w, w, scale=-0.5)

        P.dma("pool", cst[:], cst_d, writes=["cst"], key="cst")
        P.dma("sp", cstf[:], cstf_d, writes=["cstf"], key="cstf")
        P.dma("sp", pv[:, 0:NPV], pv_d, writes=["pv"], key="pv")
        P.dma("sp", pbc[:], pb_d, writes=["pbc"], key="pbc")
        P.dma("pool", wlora[:], wlora_d, writes=["wlora"], key="wlora")
        P.dma("pool", wgu[:], wgu_d, writes=["wgu"], key="wgu")
        LB, OML, NOML = NPV, NPV + 4, NPV + 8
        tt(pv[:, LB:LB + 4], pv[:, PV_L1:PV_L1 + 4], pv[:, PV_L0:PV_L0 + 4], ALU.subtract, ["pv"], ["pv"])
        act(pv[:, LB:LB + 4], pv[:, LB:LB + 4], AF.Exp, ["pv"], ["pv"])
        ts(pv[:, LB:LB + 4], pv[:, LB:LB + 4], 1.0, None, ALU.add, None, ["pv"], ["pv"])
        P.op("dve", lambda e: e.reciprocal(pv[:, LB:LB + 4], pv[:, LB:LB + 4]), ["pv"], ["pv"])
        ts(pv[:, OML:OML + 4], pv[:, LB:LB + 4], -1.0, 1.0, ALU.mult, ALU.add, ["pv"], ["pv"])
        ts(pv[:, NOML:NOML + 4], pv[:, OML:OML + 4], -1.0, None, ALU.mult, None, ["pv"], ["pv"])
        P.op("dve", lambda e: e.memset(Sh[:], 0.0), [], ["Sh"])
        P.op("dve", lambda e: e.memset(Shb[0][:], 0.0), [], ["Shb0"])
        P.op("dve", lambda e: e.memset(Hr[:], 0.0), [], ["Hr"])
        P.op("dve", lambda e: e.memset(Hrb[0][:], 0.0), [], ["Hrb0"])
        P.op("dve", lambda e: e.memset(carry[:], 0.0), [], ["carry"])
        state = {"shp": 0, "hrp": 0, "piece": 0}

        def piece_dmas(pi):
            if pi < 6:
                lo = [0, 1024, 2048, 3072, 3840, 4864][pi]
                n = [1024, 1024, 1024, 768, 1024, 1024][pi]
                return [(lambda R, n=n: R[:, 0:8 * n].rearrange("p (k c) -> p k c", k=8), w_in_v[:, :, lo:lo + n])]
            if pi == 6:
                return [(lambda R: R[:, 0:4096].rearrange("p (k c) -> p k c", k=4), w_ao_v),
                        (lambda R: R[:, 4096:8192].rearrange("p (k c) -> p k c", k=4), w_bo_v)]
            if pi == 7:
                return [(lambda R: R[:, 0:8192].rearrange("p (k c) -> p k c", k=8), w_o_v)]
            e8 = pi - 8
            return [(lambda R: R[:, 0:4096].rearrange("p (k c) -> p k c", k=8), w_up_v[:, :, e8 * 512:(e8 + 1) * 512]),
                    (lambda R: R[:, 4096:8192].rearrange("p (k c) -> p k c", k=4), w_dn_v[:, e8 * 4:(e8 + 1) * 4, :])]

        issued = {"n": 0}
        NPIECE = 16
        total_pieces = NPIECE * len(blocks)

        def ensure_issued(upto):
            while issued["n"] <= min(upto, total_pieces - 1):
                g = issued["n"]
                slot = g % 4
                for vf, src in piece_dmas(g % NPIECE):
                    P.dma("pool", vf(ring[slot]), src, writes=[f"ring{slot}"], key=f"ring{slot}")
                issued["n"] += 1

        def use_piece(bi, pi):
            g = bi * NPIECE + pi
            ensure_issued(g)
            return ring[g % 4], f"ring{g % 4}"

        def done_piece(bi, pi):
            ensure_issued(bi * NPIECE + pi + 4)

        ensure_issued(3)

        for bi, tiles in enumerate(blocks):
            ptiles = [t for t in tiles if t < 16]
            has_s = 16 in tiles
            npc = 128 * len(ptiles)
            SC0 = npc
            HC0 = npc + 64
            ncol = npc + (80 if has_s else 0)
            ttiles = [(t, i * 128, 128) for i, t in enumerate(ptiles)] + ([(16, SC0, 64)] if has_s else [])

            P.barrier()
            A = Arena()
            xin = [A.f32(1024) for _ in range(2)]
            hb = [A.bf(1024) for _ in range(2)]
            junk = A.bf(1024)
            hf = A.f32(1024)
            for i, (t, c0, rows) in enumerate(ttiles):
                xi, hbi = xin[i % 2], hb[i % 2]
                src = xp[t * 128:(t + 1) * 128, :] if t < 16 else xs
                P.dma("sp", xi[0:rows, :], src, writes=[f"xin{i%2}"], key=f"xin{i%2}")
                act(junk[0:rows, :], xi[0:rows, :], AF.Square, [f"xin{i%2}"], ["junk", "ssq"], accum_out=small[0:rows, 0:1])
                rsqrt_small(small[0:rows, 1:2], small[0:rows, 0:1], 1.0 / D, NORM_EPS, ["ssq"], ["rstd"])
                stt(hbi[0:rows, :], xi[0:rows, :], small[0:rows, 1:2], pbc[0:rows, PB_GMIX:PB_GMIX + D], ALU.mult, ALU.mult,
                    [f"xin{i%2}", "rstd", "pbc"], [f"hb{i%2}"])
                if t >= last_pt:
                    stt(hf[0:rows, :], xi[0:rows, :], small[0:rows, 1:2], pbc[0:rows, PB_GMIX:PB_GMIX + D], ALU.mult, ALU.mult,
                        [f"xin{i%2}", "rstd", "pbc"], ["hf"])
                    if t == last_pt:
                        P.dma("sp", sh_p, hf[127:128, :], reads=["hf"], key="o_shp", final=True)
                    else:
                        P.dma("sp", sh_s, hf[3:64:4, :], reads=["hf"], key="o_shs", final=True)
                bank = i % 2
                for k in range(8):
                    tr(psb[bank][:, k * 128:k * 128 + rows], hbi[0:rows, k * 128:(k + 1) * 128], ident[0:rows, 0:rows],
                       [f"hb{i%2}", "cst"], [f"ps{bank}"])
                cp(hT[:, :, c0:c0 + rows], psb[bank].rearrange("p (k c) -> p k c", k=8)[:, :, 0:rows], [f"ps{bank}"], ["hT"], eng="act")
            if has_s:
                P.dma("sp", xin[0][0:16, :], shs, writes=["xin0"], key="xin0")
                cp(hb[0][0:16, :], xin[0][0:16, :], ["xin0"], ["hb0"])
                for k in range(8):
                    tr(psb[0][:, k * 128:k * 128 + 16], hb[0][0:16, k * 128:(k + 1) * 128], ident[0:16, 0:16], ["hb0", "cst"], ["ps0"])
                cp(hT[:, :, HC0:HC0 + 16], psb[0].rearrange("p (k c) -> p k c", k=8)[:, :, 0:16], ["ps0"], ["hT"], eng="act")

            if stop_after == 'P0':
                break
            P.barrier()
            A = Arena()
            qtT = A.bf(4 * MAXC).rearrange("p (j c) -> p j c", j=4)
            ktT = A.bf(4 * MAXC).rearrange("p (j c) -> p j c", j=4)
            khT = A.bf(4 * MAXC).rearrange("p (j c) -> p j c", j=4)
            gam = A.f32(4 * 32).rearrange("p (j c) -> p j c", j=4)
            sg4 = A.f32(2048).rearrange('p (j c) -> p j c', j=4); Ee4 = A.f32(2048).rearrange('p (j c) -> p j c', j=4)
            lg = A.f32(512); kf = A.f32(512); bb = A.f32(512); Ei = A.f32(512); qs = [A.f32(512), A.f32(512)]
            v_tm = A.bf(512); kh_tm = A.bf(512); gs = A.f32(512); att_bf = A.bf(512).rearrange("p (j c) -> p j c", j=4)
            otmp = A.f32(512); oa_tm = A.bf(512)
            s0f = A.f32(2048).rearrange("p (b v) -> p b v", b=16); s0b = A.bf(2048).rearrange("p (b v) -> p b v", b=16)
            qmk = A.bf(1024).rearrange("p (b t) -> p b t", b=16); vbd = A.bf(2048).rearrange("p (b v) -> p b v", b=16)
            if bi == len(blocks) - 1:
                print('P1 arena words', A.off)
            W0, W0s = use_piece(bi, 0)
            wA0 = W0[:, 0:8192].rearrange("p (k c) -> p k c", k=8)
            groups = [(c, min(c + 512, npc), False) for c in range(0, npc, 512)] + ([(SC0, SC0 + 64, True)] if has_s else [])
            chunk_base = {}
            cb = 0
            for (c0, c1, is_s) in groups:
                n = c1 - c0
                csz = 4 if is_s else 64
                nch = n // csz
                cmask = cstf[:, 512:576] if is_s else cstf[:, 0:n]
                chunk_base[c0] = cb
                for j in range(4):
                    pa = pst[j % 2]
                    for k in range(8):
                        mm(pa[:, 0:n], wA0[:, k, 512 + j * 128:512 + (j + 1) * 128], hT[:, k, c0:c1], k == 0, k == 7, ["hT", W0s], [f"ps{j%2}"])
                    act(sg4[:, j, 0:n], pa[:, 0:n], AF.Sigmoid, [f"ps{j%2}"], [f"sg{j}"])
                for j in range(4):
                    act(lg[:, 0:n], sg4[:, j, 0:n], AF.Ln, [f"sg{j}", "pv"], ["lg"], bias=pv[:, LB + j:LB + j + 1], scale=pv[:, OML + j:OML + j + 1])
                    ts(kf[:, 0:n], sg4[:, j, 0:n], pv[:, NOML + j:NOML + j + 1], pv[:, OML + j:OML + j + 1], ALU.mult, ALU.add, [f"sg{j}", "pv"], ["kf"])
                    scan(bb[:, 0:n], cmask, lg[:, 0:n], ["lg", "cstf"], ["bb"])
                    act(Ee4[:, j, 0:n], bb[:, 0:n], AF.Exp, ["bb"], [f"Ee{j}"])
                    act(Ei[:, 0:n], bb[:, 0:n], AF.Exp, ["bb"], ["Ei"], scale=-1.0)
                    tt(ktT[:, j, c0:c1], kf[:, 0:n], Ei[:, 0:n], ALU.mult, ["kf", "Ei"], ["ktT"])
                    Ev = Ee4[:, j, 0:n].rearrange("p (c t) -> p c t", t=csz)
                    tt(khT[:, j, c0:c1].rearrange("p (c t) -> p c t", t=csz), ktT[:, j, c0:c1].rearrange("p (c t) -> p c t", t=csz),
                       Ev[:, :, csz - 1:csz].to_broadcast([128, nch, csz]), ALU.mult, ["ktT", f"Ee{j}"], ["khT"])
                    cp(gam[:, j, cb:cb + nch], Ev[:, :, csz - 1], [f"Ee{j}"], ["gam"])
                for j in range(4):
                    pq = pst[2 + j % 2]
                    for k in range(8):
                        mm(pq[:, 0:n], wA0[:, k, j * 128:(j + 1) * 128], hT[:, k, c0:c1], k == 0, k == 7, ["hT", W0s], [f"ps{2+j%2}"])
                    act(qs[j % 2][:, 0:n], pq[:, 0:n], AF.Silu, [f"ps{2+j%2}"], [f"qs{j%2}"])
                    tt(qtT[:, j, c0:c1], qs[j % 2][:, 0:n], Ee4[:, j, 0:n], ALU.mult, [f"qs{j%2}", f"Ee{j}"], ["qtT"])
                cb += nch
            done_piece(bi, 0)
            W1, W1s = use_piece(bi, 1)
            wA1 = W1[:, 0:8192].rearrange("p (k c) -> p k c", k=8)
            mIT_p = cst[:, C_MIT:C_MIT + 128]
            for (t, c0, rows) in ttiles:
                is_s = t == 16
                R0 = slice(0, rows)
                for k in range(8):
                    mm(pst[0][R0, :], hT[:, k, c0:c0 + rows], wA1[:, k, 0:512], k == 0, k == 7, ["hT", W1s], ["ps0"])
                cp(v_tm[R0, :], pst[0][R0, :], ["ps0"], ["v_tm"], eng="act")
                for k in range(8):
                    mm(pst[1][R0, :], hT[:, k, c0:c0 + rows], wA1[:, k, 512:1024], k == 0, k == 7, ["hT", W1s], ["ps1"])
                act(gs[R0, :], pst[1][R0, :], AF.Silu, ["ps1"], ["gs"])
                tt(gs[R0, :], gs[R0, :], pbc[R0, PB_HNW:PB_HNW + 512], ALU.mult, ["gs", "pbc"], ["gs"])
                for j in range(4):
                    tr(psb[2][R0, j * 128:(j + 1) * 128], khT[:, j, c0:c0 + rows], ident, ["khT", "cst"], ["ps2"])
                cp(kh_tm[R0, :], psb[2][R0, 0:512], ["ps2"], ["kh_tm"], eng="act")
                p3 = pst[3][:].rearrange("p (j c) -> p j c", j=4)
                for j in range(4):
                    mm(p3[R0, j, 0:rows], ktT[:, j, c0:c0 + rows], qtT[:, j, c0:c0 + rows], True, True, ["ktT", "qtT"], ["ps3"])
                msk = cst[R0, C_SIT:C_SIT + 64] if is_s else mIT_p
                tt(att_bf[R0, :, 0:rows], p3[R0, :, 0:rows], msk.rearrange("p (o c) -> p o c", o=1).to_broadcast([rows, 4, rows]), ALU.mult,
                   ["ps3", "cst"], ["att_bf"])
                p4 = pst[4][:].rearrange("p (j c) -> p j c", j=4)
                if not is_s:
                    cbase = chunk_base[(c0 // 512) * 512] + (c0 % 512) // 64
                    p5 = pst[5][:].rearrange("p (j c) -> p j c", j=4)
                    p6 = pst[6][:].rearrange("p (j c) -> p j c", j=4)
                    for ci, pp, pn in ((0, p5, "ps5"), (1, p6, "ps6")):
                        rs = slice(ci * 64, ci * 64 + 64)
                        for j in range(4):
                            mm(pp[:, j, :], kh_tm[rs, j * 128:(j + 1) * 128], v_tm[rs, j * 128:(j + 1) * 128], True, True, ["kh_tm", "v_tm"], [pn])
                    for ci, pp, pn in ((0, p5, "ps5"), (1, p6, "ps6")):
                        rs = slice(ci * 64, ci * 64 + 64)
                        cur = state["shp"]
                        for j in range(4):
                            mm(p4[rs, j, :], qtT[:, j, c0 + ci * 64:c0 + ci * 64 + 64], Shb[cur][:, j, :], True, False, ["qtT", f"Shb{cur}"], ["ps4"])
                            mm(p4[rs, j, :], att_bf[:, j, ci * 64:ci * 64 + 64], v_tm[:, j * 128:(j + 1) * 128], False, True, ["att_bf", "v_tm"], ["ps4"])
                        for j in range(4):
                            stt(Sh[:, j, :], Sh[:, j, :], gam[:, j, cbase + ci:cbase + ci + 1], pp[:, j, :], ALU.mult, ALU.add, ["Sh", "gam", pn], ["Sh"])
                        nxt = 1 - cur
                        cp(Shb[nxt][:], Sh[:], ["Sh"], [f"Shb{nxt}"], eng="act")
                        state["shp"] = nxt
                else:
                    sb0 = chunk_base[SC0]
                    for j in range(4):
                        P.dma("sp", s0f[:], shg[:, j, :, :].rearrange("b k v -> k b v"), writes=["s0f"], key="s0f")
                        cp(s0b[:], s0f[:], ["s0f"], ["s0b"], eng="act")
                        tt(qmk[:], qtT[:, j, c0:c0 + 64].rearrange("p (o t) -> p o t", o=1).to_broadcast([128, 16, 64]),
                           cst[:, C_SELQ:C_SELQ + 1024].rearrange("p (b t) -> p b t", b=16), ALU.mult, ["qtT", "cst"], ["qmk"])
                        for b in range(16):
                            mm(p4[0:64, j, :], qmk[:, b, :], s0b[:, b, :], b == 0, False, ["qmk", "s0b"], ["ps4"])
                        mm(p4[0:64, j, :], att_bf[0:64, j, 0:64], v_tm[0:64, j * 128:(j + 1) * 128], False, True, ["att_bf", "v_tm"], ["ps4"])
                        tt(vbd[0:64], v_tm[0:64, j * 128:(j + 1) * 128].rearrange("p (o v) -> p o v", o=1).to_broadcast([64, 16, 128]),
                           cst[0:64, C_SELV:C_SELV + 16].rearrange("p (b o) -> p b o", o=1).to_broadcast([64, 16, 128]), ALU.mult, ["v_tm", "cst"], ["vbd"])
                        tt(s0f[:], s0f[:], gam[:, j, sb0:sb0 + 16].rearrange("p (b o) -> p b o", o=1).to_broadcast([128, 16, 128]), ALU.mult,
                           ["s0f", "gam", "s0b"], ["s0f"])
                        for q in range(4):
                            pq_ = pst[5 + q % 2]
                            mm(pq_[:, :], kh_tm[0:64, j * 128:(j + 1) * 128], vbd[0:64, q * 4:(q + 1) * 4, :], True, True, ["kh_tm", "vbd"], [f"ps{5+q%2}"])
                            tt(s0f[:, q * 4:(q + 1) * 4, :], s0f[:, q * 4:(q + 1) * 4, :], pq_[:].rearrange("p (b v) -> p b v", b=4), ALU.add,
                               ["s0f", f"ps{5+q%2}"], ["s0f"])
                        P.dma("sp", hg_s[:, j, :, :].rearrange("b k v -> k b v"), s0f[:], reads=["s0f"], key="o_hgs", final=True)
                for j in range(4):
                    act(otmp[R0, j * 128:(j + 1) * 128], p4[R0, j, :], AF.Square, ["ps4"], ["otmp", "ss4"], accum_out=small[R0, 8 + j:9 + j])
                rsqrt_small(small[R0, 12:16], small[R0, 8:12], 1.0 / 128, HGRN_EPS, ["ss4"], ["rs4"])
                tt(otmp[R0, :].rearrange("p (j c) -> p j c", j=4), p4[R0], small[R0, 12:16].rearrange("p (j o) -> p j o", o=1).to_broadcast([rows, 4, 128]),
                   ALU.mult, ["ps4", "rs4", "otmp"], ["otmp"])
                tt(oa_tm[R0, :], otmp[R0, :], gs[R0, :], ALU.mult, ["otmp", "gs"], ["oa_tm"])
                for j in range(4):
                    tr(psb[7][:, j * 128:j * 128 + rows], oa_tm[R0, j * 128:(j + 1) * 128], ident[R0, R0], ["oa_tm", "cst"], ["ps7"])
                cp(oT[:, 0:4, c0:c0 + rows], psb[7][:, 0:512].rearrange("p (j c) -> p j c", j=4)[:, :, 0:rows], ["ps7"], ["oT"], eng="act")
            done_piece(bi, 1)
            if bi == len(blocks) - 1:
                P.dma("sp", hg_p.rearrange("h k v -> k h v"), Sh[:], reads=["Sh"], key="o_hgp", final=True)
            if stop_after == 'P1':
                break
            import os as _os
            P.barrier()
            A = Arena()
            GN = 128
            ubj = [A.f32(GN + 1) for _ in range(2)]
            dtmp = A.f32(GN); upj = A.f32(64)
            ush = A.f32(14 * 16).rearrange("p (j c) -> p j c", j=14)
            xr = A.f32(4 * GN).rearrange("p (j c) -> p j c", j=4)
            xk = A.f32(4 * GN).rearrange("p (j c) -> p j c", j=4)
            xv = A.f32(4 * GN).rearrange("p (j c) -> p j c", j=4)
            xwa = A.f32(GN); xg = A.f32(GN)
            twa = A.bf(GN); sgd = A.bf(GN)
            sgw4 = A.f32(4 * GN).rearrange('p (j c) -> p j c', j=4); aa4 = A.f32(4 * GN).rearrange('p (j c) -> p j c', j=4)
            NB2 = 1 if has_s else 2
            cs_L = [A.f32(GN) for _ in range(NB2)]; Gt_L = [A.f32(GN) for _ in range(NB2)]; Gi_L = [A.f32(GN) for _ in range(NB2)]
            Gex_L = [A.f32(GN) for _ in range(NB2)]; kkf_L = [A.f32(GN) for _ in range(NB2)]; kk2_L = [A.bf(GN) for _ in range(NB2)]
            rn_L = [A.f32(GN) for _ in range(NB2)]; kkn_L = [A.f32(GN) for _ in range(NB2)]; t1_L = [A.f32(GN) for _ in range(NB2)]
            km_L = [A.f32(GN) for _ in range(NB2)]; ka_L = [A.f32(GN) for _ in range(NB2)]
            mk4 = lambda: A.bf(4 * GN).rearrange("p (j c) -> p j c", j=4)
            pT = mk4(); btT = mk4(); kt2 = mk4(); bhT = mk4(); khT2 = mk4(); vT = mk4()
            atT_m = [mk4(), mk4()]; rtT_m = [mk4(), mk4()]
            GtT = A.f32(4 * 64).rearrange("p (j c) -> p j c", j=4)
            gC = A.f32(4 * 16).rearrange("p (j c) -> p j c", j=4)
            v2 = A.bf(512); g_tm = A.f32(512); bsum = A.f32(8)
            bh_c = [A.bf(512), A.bf(512)]; kh_c = [A.bf(512), A.bf(512)]
            mk44 = lambda: A.bf(512).rearrange("p (h c) -> p h c", h=4)
            Lb = [mk44() for _ in range(2)]; Mb = [mk44() for _ in range(2)]; Rb = [mk44() for _ in range(2)]
            TTb = [mk44() for _ in range(2)]; AakT = [mk44() for _ in range(2)]; ArbT = [mk44() for _ in range(2)]; ArkT = [mk44() for _ in range(2)]
            Xb = A.bf(512).rearrange("p (h c) -> p h c", h=8)
            Ub = A.bf(512).rearrange("p (h c) -> p h c", h=8)
            y_tm = A.f32(512); ysq = A.f32(512); ob_tm = A.bf(512)
            st8 = A.f32(40)
            if has_s:
                s0n = A.f32(1024).rearrange("p (b k) -> p b k", b=16)
                s0c = A.bf(1024).rearrange("p (b k) -> p b k", b=16)
                hsb = A.bf(4096).rearrange("p (j b k) -> p j b k", j=4, b=16)
                atm = A.bf(1024).rearrange("p (b k) -> p b k", b=16)
                rtm = A.bf(1024).rearrange("p (b k) -> p b k", b=16)
                bdb = A.bf(1024).rearrange("p (b k) -> p b k", b=16)
                bdk = A.bf(1024).rearrange("p (b k) -> p b k", b=16)
                G_tm = A.f32(512)
            for _t, _n in ((atT_m[0], 'atT'), (atT_m[1], 'atT'), (rtT_m[0], 'rtT'), (rtT_m[1], 'rtT')):
                P.op('dve', lambda e, _t=_t: e.memset(_t, 0.0), [], [_n])
            for _t, _n in ((bh_c[0], 'bh_tm'), (bh_c[1], 'bh_tm'), (kh_c[0], 'kh2_tm'), (kh_c[1], 'kh2_tm'), (Xb, 'Xb'), (Ub, 'Ub')):
                P.op('dve', lambda e, _t=_t: e.memset(_t, 0.0), [], [_n])
            if has_s:
                P.op('dve', lambda e, _t=hsb: e.memset(_t, 0.0), [], ['hsb'])
            if bi == len(blocks) - 1:
                print('P2 arena words', A.off)
            W2, W2s = use_piece(bi, 2)
            W3, W3s = use_piece(bi, 3)
            wB0 = W2[:, 0:8192].rearrange("p (k c) -> p k c", k=8)
            wB1 = W3[:, 0:8 * 768].rearrange("p (k c) -> p k c", k=8)

            def wB(k, jb):
                if jb < 8:
                    return wB0[:, k, jb * 128:(jb + 1) * 128]
                return wB1[:, k, (jb - 8) * 128:(jb - 7) * 128]

            P2STOP = float(_os.environ.get('P2STOP', '99'))
            rgroups = [(c, min(c + GN, npc), 0) for c in range(0, npc, GN)]
            if has_s:
                rgroups = [(HC0, HC0 + 16, 2)] + rgroups + [(SC0, SC0 + 64, 1)]
            for (c0, c1, kind) in rgroups:
                n = c1 - c0
                for jb in range(14):
                    pa = pst[jb % 2]
                    for k in range(8):
                        mm(pa[:, 0:n], wB(k, jb), hT[:, k, c0:c1], k == 0, k == 7, ["hT", W2s, W3s], [f"ps{jb%2}"])
                    if kind == 2:
                        cp(ush[:, jb, :], pa[:, 0:16], [f"ps{jb%2}"], ["ush"], eng="act")
                        continue
                    u = ubj[jb % 2]
                    us = f"ubj{jb%2}"
                    cp(u[:, 1:n + 1], pa[:, 0:n], [f"ps{jb%2}"], [us], eng="act")
                    if kind == 0:
                        cp(u[:, 0:1], carry[:, jb:jb + 1], ["carry", us], [us])
                        cp(carry[:, jb:jb + 1], u[:, n:n + 1], [us, "carry"], ["carry"])
                        up_ap = u[:, 0:n]
                        ups = us
                    else:
                        cp(upj[:, 0:64].rearrange("p (b t) -> p b t", t=4)[:, :, 1:4], u[:, 1:65].rearrange("p (b t) -> p b t", t=4)[:, :, 0:3], [us], ["upj"])
                        cp(upj[:, 0:64].rearrange("p (b t) -> p b t", t=4)[:, :, 0], ush[:, jb, :], ["ush", "upj"], ["upj"])
                        up_ap = upj[:, 0:64]
                        ups = "upj"
                    tt(dtmp[:, 0:n], up_ap, u[:, 1:n + 1], ALU.subtract, [us, ups], ["dtmp"])
                    if jb < 4:
                        dst, ds = xr[:, jb, 0:n], "xr"
                    elif jb < 8:
                        dst, ds = xk[:, jb - 4, 0:n], "xk"
                    elif jb < 12:
                        dst, ds = xv[:, jb - 8, 0:n], "xv"
                    elif jb == 12:
                        dst, ds = xwa[:, 0:n], "xwa"
                    else:
                        dst, ds = xg[:, 0:n], "xg"
                    stt(dst, dtmp[:, 0:n], pv[:, PV_MU + jb:PV_MU + jb + 1], u[:, 1:n + 1], ALU.mult, ALU.add, ["dtmp", us, "pv"], [ds])
                if kind == 2:
                    continue
                if P2STOP < 1: continue
                is_s = kind == 1
                csz = 4 if is_s else 64
                nch = n // csz
                cmask = cstf[:, 512:576] if is_s else cstf[:, 0:n]
                act(twa[0:64, 0:n], xwa[0:64, 0:n], AF.Tanh, ["xwa"], ["twa"])
                cp(twa[64:128, 0:n], xwa[64:128, 0:n], ["xwa", "twa"], ["twa"])
                act(sgd[:, 0:n], xg[:, 0:n], AF.Sigmoid, ["xg"], ["sgd"])
                pd4 = pst[0][:].rearrange("p (j c) -> p j c", j=4)
                pa4 = pst[1][:].rearrange("p (j c) -> p j c", j=4)
                for j in range(4):
                    mm(pd4[:, j, 0:n], wlora[0:64, j * 128:(j + 1) * 128], twa[0:64, 0:n], True, True, ["wlora", "twa"], ["ps0"])
                    mm(pa4[:, j, 0:n], wlora[64:128, j * 128:(j + 1) * 128], twa[64:128, 0:n], True, True, ["wlora", "twa"], ["ps1"])
                for j in range(4):
                    act(sgw4[:, j, 0:n], pd4[:, j, 0:n], AF.Sigmoid, ["ps0", "pv"], [f"sgw{j}"], bias=pv[:, PV_W0 + j:PV_W0 + j + 1])
                    act(aa4[:, j, 0:n], pa4[:, j, 0:n], AF.Sigmoid, ["ps1", "pv"], [f"aa{j}"], bias=pv[:, PV_A0 + j:PV_A0 + j + 1])
                for j in range(4):
                    ib = j % NB2
                    sx = str(ib)
                    cs_, Gt, Gi, Gex, kkf, kk2 = cs_L[ib], Gt_L[ib], Gi_L[ib], Gex_L[ib], kkf_L[ib], kk2_L[ib]
                    rn, kkn, t1, km, ka_ = rn_L[ib], kkn_L[ib], t1_L[ib], km_L[ib], ka_L[ib]
                    sgw = sgw4[:, j, :]
                    aa = aa4[:, j, :]
                    sgs, aas = f"sgw{j}", f"aa{j}"
                    scan(cs_[:, 0:n], cmask, sgw[:, 0:n], [sgs, "cstf"], ["cs" + sx])
                    act(Gt[:, 0:n], cs_[:, 0:n], AF.Exp, ["cs" + sx], ["Gt" + sx], scale=-DECAY_SCALE)
                    act(Gi[:, 0:n], cs_[:, 0:n], AF.Exp, ["cs" + sx], ["Gi" + sx], scale=DECAY_SCALE)
                    tt(cs_[:, 0:n], cs_[:, 0:n], sgw[:, 0:n], ALU.subtract, ["cs" + sx, sgs, "Gt" + sx, "Gi" + sx], ["cs" + sx])
                    act(Gex[:, 0:n], cs_[:, 0:n], AF.Exp, ["cs" + sx], ["Gex" + sx], scale=-DECAY_SCALE)
                    ts(kkf[:, 0:n], xk[:, j, 0:n], pv[:, PV_KK + j:PV_KK + j + 1], None, ALU.mult, None, ["xk", "pv"], ["kkf" + sx])
                    act(kk2[:, 0:n], kkf[:, 0:n], AF.Square, ["kkf" + sx], ["kk2" + sx])
                    mm(pst[2 + ib][:, 0:n], cst[:, C_ONESBLK:C_ONESBLK + 128], kk2[:, 0:n], True, True, ["cst", "kk2" + sx], [f"ps{2+ib}"])
                    ts(rn[:, 0:n], pst[2 + ib][:, 0:n], 1e-24, None, ALU.max, None, [f"ps{2+ib}"], ["rn" + sx])
                    act(rn[:, 0:n], rn[:, 0:n], AF.Ln, ["rn" + sx], ["rn" + sx])
                    act(rn[:, 0:n], rn[:, 0:n], AF.Exp, ["rn" + sx], ["rn" + sx], scale=-0.5)
                    tt(kkn[:, 0:n], kkf[:, 0:n], rn[:, 0:n], ALU.mult, ["kkf" + sx, "rn" + sx], ["kkn" + sx])
                    ts(t1[:, 0:n], aa[:, 0:n], -1.0, None, ALU.add, None, [aas], ["t1" + sx])
                    ts(t1[:, 0:n], t1[:, 0:n], pv[:, PV_KA + j:PV_KA + j + 1], None, ALU.mult, None, ["t1" + sx, "pv"], ["t1" + sx])
                    stt(km[:, 0:n], t1[:, 0:n], 1.0, xk[:, j, 0:n], ALU.add, ALU.mult, ["t1" + sx, "xk"], ["km" + sx])
                    stt(pT[:, j, 0:n], xr[:, j, 0:n], pv[:, PV_RK + j:PV_RK + j + 1], km[:, 0:n], ALU.mult, ALU.mult, ["xr", "pv", "km" + sx], ["pT"])
                    tt(atT_m[0][0:64, j, 0:n], kkn[0:64, 0:n], Gex[0:64, 0:n], ALU.mult, ["kkn" + sx, "Gex" + sx], ["atT"])
                    tt(atT_m[1][64:128, j, 0:n], kkn[64:128, 0:n], Gex[64:128, 0:n], ALU.mult, ["kkn" + sx, "Gex" + sx, "atT"], ["atT"])
                    tt(ka_[:, 0:n], kkn[:, 0:n], aa[:, 0:n], ALU.mult, ["kkn" + sx, aas], ["ka_" + sx])
                    stt(btT[:, j, 0:n], ka_[:, 0:n], -1.0, Gi[:, 0:n], ALU.mult, ALU.mult, ["ka_" + sx, "Gi" + sx], ["btT"])
                    tt(kt2[:, j, 0:n], km[:, 0:n], Gi[:, 0:n], ALU.mult, ["km" + sx, "Gi" + sx], ["kt2"])
                    tt(rtT_m[0][0:64, j, 0:n], xr[0:64, j, 0:n], Gt[0:64, 0:n], ALU.mult, ["xr", "Gt" + sx], ["rtT"])
                    tt(rtT_m[1][64:128, j, 0:n], xr[64:128, j, 0:n], Gt[64:128, 0:n], ALU.mult, ["xr", "Gt" + sx, "rtT"], ["rtT"])
                    Gv = Gt[:, 0:n].rearrange("p (c t) -> p c t", t=csz)
                    gcb = Gv[:, :, csz - 1:csz].to_broadcast([128, nch, csz])
                    tt(bhT[:, j, 0:n].rearrange("p (c t) -> p c t", t=csz), btT[:, j, 0:n].rearrange("p (c t) -> p c t", t=csz), gcb, ALU.mult, ["btT", "Gt" + sx], ["bhT"])
                    tt(khT2[:, j, 0:n].rearrange("p (c t) -> p c t", t=csz), kt2[:, j, 0:n].rearrange("p (c t) -> p c t", t=csz), gcb, ALU.mult, ["kt2", "Gt" + sx], ["khT2"])
                    cp(gC[:, j, 0:nch], Gv[:, :, csz - 1], ["Gt" + sx], ["gC"])
                    if is_s:
                        cp(GtT[:, j, :], Gt[:, 0:64], ["Gt" + sx], ["GtT"])
                    cp(vT[:, j, 0:n], xv[:, j, 0:n], ["xv"], ["vT"], eng="act")
                if P2STOP < 2: continue
                rows = n
                R0 = slice(0, rows)
                cc = slice(0, rows)
                for j in range(4):
                    tr(psb[0][R0, j * 128:(j + 1) * 128], vT[:, j, cc], ident, ["vT", "cst"], ["ps0"])
                    tr(psb[1][R0, j * 128:(j + 1) * 128], bhT[:, j, cc], ident, ["bhT", "cst"], ["ps1"])
                    tr(psb[2][R0, j * 128:(j + 1) * 128], khT2[:, j, cc], ident, ["khT2", "cst"], ["ps2"])
                cp(v2[R0, :], psb[0][R0, 0:512], ["ps0"], ["v2"], eng="act")
                cp(bh_c[0][0:64, :], psb[1][0:64, 0:512], ["ps1"], ["bh_tm"], eng="act")
                cp(kh_c[0][0:64, :], psb[2][0:64, 0:512], ["ps2"], ["kh2_tm"], eng="act")
                if rows == 128:
                    cp(bh_c[1][64:128, :], psb[1][64:128, 0:512], ["ps1", "bh_tm"], ["bh_tm"], eng="act")
                    cp(kh_c[1][64:128, :], psb[2][64:128, 0:512], ["ps2", "kh2_tm"], ["kh2_tm"], eng="act")
                bh_tm, kh2_tm = bh_c[0], kh_c[0]
                if P2STOP < 2.5: continue
                mm(pst[3][R0, :], sgd[:, cc], wgu[:, :], True, True, ["sgd", "wgu"], ["ps3"])
                cp(g_tm[R0, :], pst[3][R0, :], ["ps3"], ["g_tm"], eng="act")
                if P2STOP < 2.8: continue
                for j in range(4):
                    mm(pst[4][R0, 0:64], pT[:, j, cc], cst[:, C_HSEL4 + j * 64:C_HSEL4 + (j + 1) * 64], j == 0, j == 3, ["pT", "cst"], ["ps4"])
                cp(bsum[R0, :], pst[4][R0, 0:8], ["ps4"], ["bsum"])
                if P2STOP < 3: continue
                mST = cst[R0, C_SST:C_SST + 64] if is_s else cst[:, C_MST:C_MST + 128]
                mITm = cst[R0, C_SIT:C_SIT + 64] if is_s else cst[:, C_MIT:C_MIT + 128]
                mLm = cst[R0, C_SL:C_SL + 64] if is_s else cst[:, C_ML:C_ML + 128]
                bc4 = lambda m: m.rearrange("p (o c) -> p o c", o=1).to_broadcast([rows, 4, rows])

                def fma(bufm, h):
                    return bufm[h % 2][:, h // 2, cc]

                def fmf(buf, h):
                    return buf[:, h // 2, cc]
                p0 = pst[0][:].rearrange("p (h c) -> p h c", h=4)
                p1 = pst[1][:].rearrange("p (h c) -> p h c", h=4)
                p2 = pst[2][:].rearrange("p (h c) -> p h c", h=4)
                p3 = pst[3][:].rearrange("p (h c) -> p h c", h=4)
                p4 = pst[4][:].rearrange("p (h c) -> p h c", h=4)
                for hg in range(2):
                    heads = [hg + 2 * q for q in range(4)]
                    for q, h in enumerate(heads):
                        mm(p0[R0, q, 0:rows], fma(atT_m, h), fmf(btT, h), True, True, ["atT", "btT"], ["ps0"])
                        mm(p1[R0, q, 0:rows], fmf(btT, h), fma(atT_m, h), True, True, ["atT", "btT"], ["ps1"])
                        mm(p2[R0, q, 0:rows], fmf(kt2, h), fma(atT_m, h), True, True, ["atT", "kt2"], ["ps2"])
                        mm(p3[R0, q, 0:rows], fmf(btT, h), fma(rtT_m, h), True, True, ["rtT", "btT"], ["ps3"])
                        mm(p4[R0, q, 0:rows], fmf(kt2, h), fma(rtT_m, h), True, True, ["rtT", "kt2"], ["ps4"])
                    tt(Lb[0][R0, :, 0:rows], p0[R0, :, 0:rows], bc4(mLm), ALU.mult, ["ps0", "cst"], ["Lb0"])
                    tt(Mb[0][R0, :, 0:rows], p1[R0, :, 0:rows], bc4(mST), ALU.mult, ["ps1", "cst"], ["Mb0"])
                    tt(AakT[hg][R0, :, 0:rows], p2[R0, :, 0:rows], bc4(mST), ALU.mult, ["ps2", "cst"], [f"AakT{hg}"])
                    tt(ArbT[hg][R0, :, 0:rows], p3[R0, :, 0:rows], bc4(mITm), ALU.mult, ["ps3", "cst"], [f"ArbT{hg}"])
                    tt(ArkT[hg][R0, :, 0:rows], p4[R0, :, 0:rows], bc4(mITm), ALU.mult, ["ps4", "cst"], [f"ArkT{hg}"])
                    nlev = 1 if is_s else 5
                    cur = 0
                    Rcur, Rs = Rb[0], "Rb0"
                    tt(Rb[0][R0, :, 0:rows], Mb[0][R0, :, 0:rows], bc4(ident[R0, R0]), ALU.add, ["Mb0", "cst"], ["Rb0"])
                    for lev in range(nlev):
                        nx = 1 - cur
                        for q in range(4):
                            mm(p0[R0, q, 0:rows], Mb[cur][R0, q, 0:rows], Lb[cur][R0, q, 0:rows], True, True, [f"Lb{cur}", f"Mb{cur}"], ["ps0"])
                            mm(p1[R0, q, 0:rows], Lb[cur][R0, q, 0:rows], Mb[cur][R0, q, 0:rows], True, True, [f"Lb{cur}", f"Mb{cur}"], ["ps1"])
                        cp(Lb[nx][R0, :, 0:rows], p0[R0, :, 0:rows], ["ps0"], [f"Lb{nx}"], eng="act")
                        cp(Mb[nx][R0, :, 0:rows], p1[R0, :, 0:rows], ["ps1"], [f"Mb{nx}"], eng="act")
                        for q in range(4):
                            mm(p2[R0, q, 0:rows], Lb[nx][R0, q, 0:rows], Rcur[R0, q, 0:rows], True, True, [f"Lb{nx}", Rs], ["ps2"])
                        last = lev == nlev - 1
                        Rn, Rns = (TTb[hg], f"TTb{hg}") if last else (Rb[nx], f"Rb{nx}")
                        tt(Rn[R0, :, 0:rows], p2[R0, :, 0:rows], Rcur[R0, :, 0:rows], ALU.add, ["ps2", Rs], [Rns])
                        Rcur, Rs = Rn, Rns
                        cur = nx
                if P2STOP < 4: continue
                px = pst[5][:].rearrange("p (h c) -> p h c", h=8)
                pu = pst[6][:].rearrange("p (h c) -> p h c", h=8)
                py = pst[7][:].rearrange("p (h c) -> p h c", h=8)
                ph = pst[3][:, 0:256].rearrange("p (j c) -> p j c", j=4)
                hsl = lambda h: slice((h % 2) * 64, (h % 2) * 64 + 64)
                if not is_s:
                    for ci in range(2):
                        rs = slice(ci * 64, ci * 64 + 64)
                        hc = state["hrp"]
                        ccs = slice(ci * 64, ci * 64 + 64)
                        for h in range(8):
                            q, hg = h // 2, h % 2
                            mm(px[rs, h, :], atT_m[hg][:, h // 2, ccs], Hrb[hc][:, h // 2, :], True, False, ["atT", f"Hrb{hc}"], ["ps5"])
                            mm(px[rs, h, :], AakT[hg][:, q, rs], v2[:, h * 64:(h + 1) * 64], False, True, [f"AakT{hg}", "v2"], ["ps5"])
                        cp(Xb[rs], px[rs], ["ps5"], ["Xb"], eng="act")
                        for h in range(8):
                            q, hg = h // 2, h % 2
                            mm(pu[rs, h, :], TTb[hg][:, q, rs], Xb[:, h, :], True, True, [f"TTb{hg}", "Xb"], ["ps6"])
                        cp(Ub[rs], pu[rs], ["ps6"], ["Ub"], eng="act")
                        for h in range(8):
                            q, hg = h // 2, h % 2
                            mm(py[rs, h, :], rtT_m[hg][:, h // 2, ccs], Hrb[hc][:, h // 2, :], True, False, ["rtT", f"Hrb{hc}"], ["ps7"])
                            mm(py[rs, h, :], ArbT[hg][:, q, rs], Ub[:, h, :], False, False, [f"ArbT{hg}", "Ub"], ["ps7"])
                            mm(py[rs, h, :], ArkT[hg][:, q, rs], v2[:, h * 64:(h + 1) * 64], False, True, [f"ArkT{hg}", "v2"], ["ps7"])
                            mm(ph[hsl(h), h // 2, :], bh_c[ci][:, h * 64:(h + 1) * 64], Ub[:, h, :], True, False, ["bh_tm", "Ub"], ["ps3"])
                            mm(ph[hsl(h), h // 2, :], kh_c[ci][:, h * 64:(h + 1) * 64], v2[:, h * 64:(h + 1) * 64], False, True, ["kh2_tm", "v2"], ["ps3"])
                        for jj in range(4):
                            stt(Hr[:, jj, :], Hr[:, jj, :], gC[:, jj, ci:ci + 1], ph[:, jj, :], ALU.mult, ALU.add, ["Hr", "gC", "ps3"], ["Hr"])
                        nxt = 1 - hc
                        cp(Hrb[nxt][:], Hr[:], ["Hr"], [f"Hrb{nxt}"], eng="act")
                        state["hrp"] = nxt
                else:
                    selq = lambda sl: cst[sl, C_SELQ:C_SELQ + 1024].rearrange("p (b t) -> p b t", b=16)
                    for h in range(8):
                        q, hg = h // 2, h % 2
                        pb_ = hsl(h)
                        jj = h // 2
                        P.dma("sp", s0n[0:64], swk[:, h, :, :].rearrange("b i k -> i b k"), writes=["s0n"], key="s0n")
                        cp(s0c[0:64], s0n[0:64], ["s0n"], ["s0c"], eng="act")
                        pt = psb[4][:].rearrange("p (b i) -> p b i", b=16)
                        for b in range(16):
                            tr(pt[pb_, b, :], s0c[0:64, b, :], ident[0:64, 0:64], ["s0c", "cst"], ["ps4"])
                        cp(hsb[pb_, jj], pt[pb_], ["ps4"], ["hsb"], eng="act")
                        tt(atm[:], atT_m[hg][:, jj, 0:64].rearrange("p (o t) -> p o t", o=1).to_broadcast([128, 16, 64]), selq(slice(0, 128)), ALU.mult, ["atT", "cst"], ["atm"])
                        for b in range(16):
                            mm(px[0:64, h, :], atm[:, b, :], hsb[:, jj, b, :], b == 0, False, ["atm", "hsb"], ["ps5"])
                        mm(px[0:64, h, :], AakT[hg][0:64, q, 0:64], v2[0:64, h * 64:(h + 1) * 64], False, True, [f"AakT{hg}", "v2"], ["ps5"])
                    cp(Xb[0:64], px[0:64], ["ps5"], ["Xb"], eng="act")
                    for h in range(8):
                        q, hg = h // 2, h % 2
                        mm(pu[0:64, h, :], TTb[hg][0:64, q, 0:64], Xb[0:64, h, :], True, True, [f"TTb{hg}", "Xb"], ["ps6"])
                    cp(Ub[0:64], pu[0:64], ["ps6"], ["Ub"], eng="act")
                    for j in range(4):
                        tr(pst[3][0:64, j * 128:(j + 1) * 128], GtT[:, j, :], identf, ["GtT", "cstf"], ["ps3"])
                    cp(G_tm[0:64, :], pst[3][0:64, :], ["ps3"], ["G_tm"], eng="act")
                    selv = cst[0:64, C_SELV:C_SELV + 16].rearrange("p (b o) -> p b o", o=1).to_broadcast([64, 16, 64])
                    esel = cstf[0:64, 768:1792].rearrange("p (b i) -> p b i", b=16)
                    for h in range(8):
                        q, hg = h // 2, h % 2
                        pb_ = hsl(h)
                        jj = h // 2
                        tt(rtm[:], rtT_m[hg][:, jj, 0:64].rearrange("p (o t) -> p o t", o=1).to_broadcast([128, 16, 64]), selq(slice(0, 128)), ALU.mult, ["rtT", "cst"], ["rtm"])
                        for b in range(16):
                            mm(py[0:64, h, :], rtm[:, b, :], hsb[:, jj, b, :], b == 0, False, ["rtm", "hsb"], ["ps7"])
                        mm(py[0:64, h, :], ArbT[hg][0:64, q, 0:64], Ub[0:64, h, :], False, False, [f"ArbT{hg}", "Ub"], ["ps7"])
                        mm(py[0:64, h, :], ArkT[hg][0:64, q, 0:64], v2[0:64, h * 64:(h + 1) * 64], False, True, [f"ArkT{hg}", "v2"], ["ps7"])
                        P.dma("sp", s0n[0:64], swk[:, h, :, :].rearrange("b i k -> i b k"), writes=["s0n"], key="s0n")
                        tt(bdb[0:64], bh_tm[0:64, h * 64:(h + 1) * 64].rearrange("p (o k) -> p o k", o=1).to_broadcast([64, 16, 64]), selv, ALU.mult, ["bh_tm", "cst"], ["bdb"])
                        tt(bdk[0:64], kh2_tm[0:64, h * 64:(h + 1) * 64].rearrange("p (o k) -> p o k", o=1).to_broadcast([64, 16, 64]), selv, ALU.mult, ["kh2_tm", "cst"], ["bdk"])
                        for half in range(2):
                            bs = slice(half * 8, half * 8 + 8)
                            pg = pst[0 + half]
                            for b in range(8):
                                mm(pg[0:64, b * 64:(b + 1) * 64], esel[:, half * 8 + b, :], G_tm[0:64, h * 64:(h + 1) * 64], True, True, ["cstf", "G_tm"], [f"ps{half}"])
                            pq_ = pst[2] if half == 0 else pst[4]
                            pqs = "ps2" if half == 0 else "ps4"
                            mm(pq_[0:64, :], Ub[0:64, h, :], bdb[0:64, bs, :], True, False, ["Ub", "bdb"], [pqs])
                            mm(pq_[0:64, :], v2[0:64, h * 64:(h + 1) * 64], bdk[0:64, bs, :], False, True, ["v2", "bdk"], [pqs])
                            sv = s0n[0:64, bs, :]
                            tt(sv, sv, pg[0:64, :].rearrange("p (b k) -> p b k", b=8), ALU.mult, ["s0n", f"ps{half}"], ["s0n"])
                            tt(sv, sv, pq_[0:64, :].rearrange("p (b k) -> p b k", b=8), ALU.add, ["s0n", pqs], ["s0n"])
                        P.dma("sp", wk_s[:, h, :, :].rearrange("b i k -> i b k"), s0n[0:64], reads=["s0n"], key="o_wks", final=True)
                if P2STOP < 5: continue
                cp(y_tm[R0, :], pst[7][R0, :], ["ps7"], ["y_tm"], eng="act")
                y3 = y_tm[R0, :].rearrange("p (h c) -> p h c", h=8)
                rsum(st8[R0, 0:8], y3, ["y_tm"], ["st8a"])
                act(ysq[R0, :], y_tm[R0, :], AF.Square, ["y_tm"], ["ysq"])
                ys3 = ysq[R0, :].rearrange("p (h c) -> p h c", h=8)
                rsum(st8[R0, 8:16], ys3, ["ysq"], ["st8b"])
                ts(st8[R0, 16:24], st8[R0, 0:8], 1.0 / 64, None, ALU.mult, None, ["st8a"], ["st8m"])
                tt(st8[R0, 24:32], st8[R0, 16:24], st8[R0, 16:24], ALU.mult, ["st8m"], ["st8v"])
                stt(st8[R0, 24:32], st8[R0, 8:16], 1.0 / 64, st8[R0, 24:32], ALU.mult, ALU.subtract, ["st8b", "st8v"], ["st8v"])
                rsqrt_small(st8[R0, 32:40], st8[R0, 24:32], 1.0, GN_EPS, ["st8v"], ["st8r"])
                bc8 = lambda a: a.rearrange("p (h o) -> p h o", o=1).to_broadcast([rows, 8, 64])
                tt(y3, y3, bc8(st8[R0, 16:24]), ALU.subtract, ["y_tm", "st8m", "ysq"], ["y_tm"])
                tt(y3, y3, bc8(st8[R0, 32:40]), ALU.mult, ["y_tm", "st8r"], ["y_tm"])
                tt(y_tm[R0, :], y_tm[R0, :], pbc[R0, PB_LXW:PB_LXW + 512], ALU.mult, ["y_tm", "pbc"], ["y_tm"])
                tt(y_tm[R0, :], y_tm[R0, :], pbc[R0, PB_LXB:PB_LXB + 512], ALU.add, ["y_tm", "pbc"], ["y_tm"])
                tt(ysq[R0, :].rearrange("p (h c) -> p h c", h=8), v2[R0, :].rearrange("p (h c) -> p h c", h=8), bc8(bsum[R0, :]), ALU.mult,
                   ["v2", "bsum", "ysq"], ["ysq"])
                tt(y_tm[R0, :], y_tm[R0, :], ysq[R0, :], ALU.add, ["y_tm", "ysq"], ["y_tm"])
                tt(ob_tm[R0, :], y_tm[R0, :], g_tm[R0, :], ALU.mult, ["y_tm", "g_tm"], ["ob_tm"])
                for j in range(4):
                    tr(psb[0][:, j * 128:j * 128 + rows], ob_tm[R0, j * 128:(j + 1) * 128], ident[R0, R0], ["ob_tm", "cst"], ["ps0"])
                cp(oT[:, 4:8, c0:c0 + rows], psb[0][:, 0:512].rearrange("p (j c) -> p j c", j=4)[:, :, 0:rows], ["ps0"], ["oT"], eng="act")
            done_piece(bi, 2)
            done_piece(bi, 3)
            import os as _os
            if bi == len(blocks) - 1 and not _os.environ.get('SKIP_HRT'):
                for j in range(4):
                    tr(pst[0][0:64, j * 128:(j + 1) * 128], Hr[:, j, :], identf, ["Hr", "cstf"], ["ps0"])
                cp(y_tm[0:64, :], pst[0][0:64, :], ["ps0"], ["y_tm"], eng="act")
                P.dma("sp", wk_p.rearrange("h i k -> i h k"), y_tm[0:64, :].rearrange("p (h k) -> p h k", h=8), reads=["y_tm"], key="o_wkp", final=True)

            DBG.update({k_: v_ for k_, v_ in locals().items() if not k_.startswith('_')})
            if stop_after == 'P2':
                break
            P.barrier()
            A = Arena()
            ntt = len(ttiles)
            xres = A.f32(ntt * 1024).rearrange("p (t d) -> p t d", t=ntt)
            mT = A.bf(8 * 512).rearrange("p (k c) -> p k c", k=8)
            sga = A.f32(512); m1 = A.f32(512); m2 = A.f32(512)
            hb2 = A.bf(1024); junk2 = A.bf(1024)
            aT = A.bf(4 * 512).rearrange("p (k c) -> p k c", k=4)
            rl = A.f32(512); yo = A.f32(1024)
            for i, (t, c0, rows) in enumerate(ttiles):
                src = xp[t * 128:(t + 1) * 128, :] if t < 16 else xs
                P.dma("sp", xres[0:rows, i, :], src, writes=[f"xres{i}"], key=f"xres{i}")
            W4, W4s = use_piece(bi, 4)
            W5, W5s = use_piece(bi, 5)
            W6, W6s = use_piece(bi, 6)
            W7, W7s = use_piece(bi, 7)
            wGa = W4[:, 0:8192].rearrange("p (k c) -> p k c", k=8)
            wGb = W5[:, 0:8192].rearrange("p (k c) -> p k c", k=8)
            wao = W6[:, 0:4096].rearrange("p (k c) -> p k c", k=4)
            wbo = W6[:, 4096:8192].rearrange("p (k c) -> p k c", k=4)
            wo = W7[:, 0:8192].rearrange("p (k c) -> p k c", k=8)
            ncs = npc + (64 if has_s else 0)
            g3 = [(c, min(c + 512, ncs)) for c in range(0, ncs, 512)]
            for (c0, c1) in g3:
                n = c1 - c0
                for dj in range(8):
                    dsl = slice(dj * 128, (dj + 1) * 128)
                    for k in range(4):
                        mm(pst[0][:, 0:n], wao[:, k, dsl], oT[:, k, c0:c1], k == 0, k == 3, ["oT", W6s], ["ps0"])
                    for k in range(8):
                        mm(pst[1][:, 0:n], wGa[:, k, dsl], hT[:, k, c0:c1], k == 0, k == 7, ["hT", W4s], ["ps1"])
                    act(sga[:, 0:n], pst[1][:, 0:n], AF.Sigmoid, ["ps1"], ["sga"])
                    tt(m1[:, 0:n], sga[:, 0:n], pst[0][:, 0:n], ALU.mult, ["sga", "ps0"], ["m1"])
                    for k in range(4):
                        mm(pst[2][:, 0:n], wbo[:, k, dsl], oT[:, 4 + k, c0:c1], k == 0, k == 3, ["oT", W6s], ["ps2"])
                    for k in range(8):
                        mm(pst[3][:, 0:n], wGb[:, k, dsl], hT[:, k, c0:c1], k == 0, k == 7, ["hT", W5s], ["ps3"])
                    act(sga[:, 0:n], pst[3][:, 0:n], AF.Sigmoid, ["ps3", "m1"], ["sga"])
                    tt(m2[:, 0:n], sga[:, 0:n], pst[2][:, 0:n], ALU.mult, ["sga", "ps2"], ["m2"])
                    tt(mT[:, dj, 0:n], m1[:, 0:n], m2[:, 0:n], ALU.add, ["m1", "m2"], ["mT"])
                for i, (t, tc0, rows) in enumerate(ttiles):
                    if not (c0 <= tc0 < c1):
                        continue
                    R0 = slice(0, rows)
                    lc = slice(tc0 - c0, tc0 - c0 + rows)
                    for half in range(2):
                        pp = pst[4 + half]
                        for k in range(8):
                            mm(pp[R0, :], mT[:, k, lc], wo[:, k, half * 512:(half + 1) * 512], k == 0, k == 7, ["mT", W7s], [f"ps{4+half}"])
                        tt(xres[R0, i, half * 512:(half + 1) * 512], xres[R0, i, half * 512:(half + 1) * 512], pp[R0, :], ALU.add, [f"xres{i}", f"ps{4+half}"], [f"xres{i}"])
                    act(junk2[R0, :], xres[R0, i, :], AF.Square, [f"xres{i}"], ["junk2", "ssq"], accum_out=small[R0, 0:1])
                    rsqrt_small(small[R0, 1:2], small[R0, 0:1], 1.0 / D, NORM_EPS, ["ssq"], ["rstd"])
                    stt(hb2[R0, :], xres[R0, i, :], small[R0, 1:2], pbc[R0, PB_GMLP:PB_GMLP + D], ALU.mult, ALU.mult, [f"xres{i}", "rstd", "pbc"], ["hb2"])
                    for k in range(8):
                        tr(psb[6][:, k * 128:k * 128 + rows], hb2[R0, k * 128:(k + 1) * 128], ident[R0, R0], ["hb2", "cst"], ["ps6"])
                    cp(hT[:, :, tc0:tc0 + rows], psb[6].rearrange("p (k c) -> p k c", k=8)[:, :, 0:rows], ["ps6"], ["hT"], eng="act")
            for _p in (4, 5, 6, 7):
                done_piece(bi, _p)
            if stop_after == 'P3':
                break
            for e8 in range(8):
                W8, W8s = use_piece(bi, 8 + e8)
                wu = W8[:, 0:4096].rearrange("p (k c) -> p k c", k=8)
                wd = W8[:, 4096:8192].rearrange("p (k c) -> p k c", k=4)
                for (c0, c1) in g3:
                    n = c1 - c0
                    for fb in range(4):
                        pa = pst[fb % 2]
                        for k in range(8):
                            mm(pa[:, 0:n], wu[:, k, fb * 128:(fb + 1) * 128], hT[:, k, c0:c1], k == 0, k == 7, ["hT", W8s], [f"ps{fb%2}"])
                        act(rl[:, 0:n], pa[:, 0:n], AF.Relu, [f"ps{fb%2}"], ["rl"])
                        tt(aT[:, fb, 0:n], rl[:, 0:n], rl[:, 0:n], ALU.mult, ["rl"], ["aT"])
                    for i, (t, tc0, rows) in enumerate(ttiles):
                        if not (c0 <= tc0 < c1):
                            continue
                        R0 = slice(0, rows)
                        lc = slice(tc0 - c0, tc0 - c0 + rows)
                        for half in range(2):
                            pp = pst[4 + half]
                            for k in range(4):
                                mm(pp[R0, :], aT[:, k, lc], wd[:, k, half * 512:(half + 1) * 512], k == 0, k == 3, ["aT", W8s], [f"ps{4+half}"])
                            tt(xres[R0, i, half * 512:(half + 1) * 512], xres[R0, i, half * 512:(half + 1) * 512], pp[R0, :], ALU.add, [f"xres{i}", f"ps{4+half}"], [f"xres{i}"])
                done_piece(bi, 8 + e8)
            if stop_after == 'P4':
                break
            for i, (t, tc0, rows) in enumerate(ttiles):
                R0 = slice(0, rows)
                act(junk2[R0, :], xres[R0, i, :], AF.Square, [f"xres{i}"], ["junk2", "ssq"], accum_out=small[R0, 0:1])
                rsqrt_small(small[R0, 1:2], small[R0, 0:1], 1.0 / D, NORM_EPS, ["ssq"], ["rstd"])
                stt(yo[R0, :], xres[R0, i, :], small[R0, 1:2], pbc[R0, PB_GFIN:PB_GFIN + D], ALU.mult, ALU.mult, [f"xres{i}", "rstd", "pbc"], ["yo"])
                dst = y_p[t * 128:(t + 1) * 128, :] if t < 16 else y_s
                P.dma("sp", dst, yo[R0, :], reads=["yo"], key="o_y", final=True)
        P.emit()
    return nc


_CACHE = {}


def kernel(x_prompt, x_sample, state_hgrn, state_wkv, state_shift, norm_mix_g, w_in, mu_shift,
           w_decay0, w_decay_up, a0, w_aaa_up, w_gate_up, k_k, k_a, r_k, ln_x_w, ln_x_b,
           lb_logits, hgrn_norm_w, w_a_out, w_b_out, w_out, norm_mlp_g, w_up, w_down, norm_final_g):
    f = lambda a: np.ascontiguousarray(np.asarray(a, dtype=np.float32))
    blocks = [[0, 1, 2, 3], [4, 5, 6, 7], [8, 9, 10, 11], [12, 13, 14, 15, 16]]
    if "nc" not in _CACHE:
        _CACHE["nc"] = build_nc(blocks)
    nc = _CACHE["nc"]
    cst, cstf = _build_consts()
    fm = lambda v, nb: f(v).reshape(nb, 128).T
    pv = np.zeros((128, NPV), np.float32)
    lbl = f(lb_logits)
    pv[:, PV_L0:PV_L0 + 4] = fm(lbl[0], 4)
    pv[:, PV_L1:PV_L1 + 4] = fm(lbl[1], 4)
    pv[:, PV_MU:PV_MU + 14] = fm(f(mu_shift)[0], 14)
    pv[:, PV_W0:PV_W0 + 4] = fm(f(w_decay0)[0], 4)
    pv[:, PV_A0:PV_A0 + 4] = fm(f(a0)[0], 4)
    pv[:, PV_KK:PV_KK + 4] = fm(f(k_k)[0], 4)
    pv[:, PV_KA:PV_KA + 4] = fm(f(k_a)[0], 4)
    pv[:, PV_RK:PV_RK + 4] = fm(f(r_k)[0].reshape(-1), 4)
    pb = np.zeros((128, NPB), np.float32)
    pb[:, PB_GMIX:PB_GMIX + D] = f(norm_mix_g)[0][None, :]
    pb[:, PB_GMLP:PB_GMLP + D] = f(norm_mlp_g)[0][None, :]
    pb[:, PB_GFIN:PB_GFIN + D] = f(norm_final_g)[None, :]
    pb[:, PB_HNW:PB_HNW + 512] = f(hgrn_norm_w)[0][None, :]
    pb[:, PB_LXW:PB_LXW + 512] = f(ln_x_w)[0][None, :]
    pb[:, PB_LXB:PB_LXB + 512] = f(ln_x_b)[0][None, :]
    wlora = np.concatenate([f(w_decay_up)[0], f(w_aaa_up)[0]], axis=0)
    shared = {"w_in": f(w_in)[0], "w_a_out": f(w_a_out)[0], "w_b_out": f(w_b_out)[0], "w_out": f(w_out)[0],
              "w_up": f(w_up)[0], "w_down": f(w_down)[0], "wlora": f(wlora), "wgu": f(w_gate_up)[0],
              "pv": pv, "pb": pb, "cst": cst, "cstf": cstf}
    xpn, xsn = f(x_prompt), f(x_sample)
    shn, wkn, sfn = f(state_hgrn)[0], f(state_wkv)[0], f(state_shift)[0]
    in_maps = []
    for c in range(NCORE):
        m = dict(shared)
        m["xp"] = xpn[c]
        m["xs"] = xsn[c * NSB:(c + 1) * NSB].reshape(64, D)
        m["shs"] = sfn[c * NSB:(c + 1) * NSB]
        m["shg"] = shn[c * NSB:(c + 1) * NSB]
        m["swk"] = wkn[c * NSB:(c + 1) * NSB]
        in_maps.append(m)
    res = run_bass_kernel_spmd(nc, in_maps, core_ids=list(range(NCORE)))
    R = res.results
    y_prompt = np.stack([R[c]["y_p"] for c in range(NCORE)], 0)
    y_sample = np.concatenate([R[c]["y_s"].reshape(NSB, 4, D) for c in range(NCORE)], 0)
    hgp = np.stack([R[c]["hg_p"] for c in range(NCORE)], 0)[None]
    wkp = np.stack([R[c]["wk_p"] for c in range(NCORE)], 0)[None]
    shp = np.concatenate([R[c]["sh_p"] for c in range(NCORE)], 0)[None]
    hgs = np.concatenate([R[c]["hg_s"] for c in range(NCORE)], 0)[None]
    wks = np.concatenate([R[c]["wk_s"] for c in range(NCORE)], 0)[None]
    shs_o = np.concatenate([R[c]["sh_s"] for c in range(NCORE)], 0)[None]
    return (y_prompt.astype(np.float32), y_sample.astype(np.float32), hgp.astype(np.float32), wkp.astype(np.float32),
            shp.astype(np.float32), hgs.astype(np.float32), wks.astype(np.float32), shs_o.astype(np.float32))
```
